# Optimizing a Trainium2 kernel written in Bass

```python
import math
import jax, jax.numpy as jnp
from jax import lax
import numpy as np

D_MODEL = 1024
BATCH = 2
SEQ = 8192
DEPTH = 2

N_MIXERS = 2
RMS_EPS = 1e-6
ROPE_THETA = 10000.0
Q_BLOCK = 128
MAX_POS_OFFSET = 4096

DA_HEADS = 8
DA_QK_DIM = 64
DA_V_DIM = 2 * DA_QK_DIM
DA_WIDTH = DA_HEADS * DA_V_DIM

MLA_HEADS = 8
MLA_Q_LORA = 384
MLA_KV_LORA = 256
MLA_NOPE = 128
MLA_ROPE = 64
MLA_V = 128
MLA_WIDTH = MLA_HEADS * MLA_V
MLA_IN = MLA_Q_LORA + MLA_KV_LORA + MLA_ROPE + MLA_WIDTH

kernel_name = "hybrid_diffattn_mla_adaln_encoder"


def _rmsnorm(x, g):
    xf = x.astype(jnp.float32)
    y = xf * lax.rsqrt(jnp.mean(xf * xf, axis=-1, keepdims=True) + RMS_EPS)
    return (y * g.astype(jnp.float32)).astype(x.dtype)


def _rope(x, positions):
    d = x.shape[-1]
    half = d // 2
    inv = ROPE_THETA ** (-jnp.arange(half, dtype=jnp.float32) / half)
    ang = positions.astype(jnp.float32)[..., None] * inv
    cos = jnp.cos(ang)[:, :, None, :]
    sin = jnp.sin(ang)[:, :, None, :]
    xf = x.astype(jnp.float32)
    x1, x2 = xf[..., :half], xf[..., half:]
    out = jnp.concatenate([x1 * cos - x2 * sin, x2 * cos + x1 * sin], axis=-1)
    return out.astype(x.dtype)


def _to_blocks(t):
    b, s = t.shape[:2]
    t = t.reshape((b, s // Q_BLOCK, Q_BLOCK) + t.shape[2:])
    return jnp.moveaxis(t, 1, 0)


def _from_blocks(t):
    t = jnp.moveaxis(t, 0, 1)
    return t.reshape((t.shape[0], t.shape[1] * t.shape[2]) + t.shape[3:])


def _lambda_init(layer_idx):
    return 0.8 - 0.6 * math.exp(-0.3 * layer_idx)


def _modulation(c, ada_w, ada_b):
    mod = jax.nn.silu(c) @ ada_w + ada_b
    shift, scale, gate = jnp.split(mod[:, None, :], 3, axis=-1)
    return shift, scale, gate


def _diff_attention(h, positions, w_in, lam_q1, lam_k1, lam_q2, lam_k2, subln_g, w_out, lambda_init):
    b, s, _ = h.shape
    proj = h @ w_in
    q, k, v, z = jnp.split(proj, [DA_WIDTH, 2 * DA_WIDTH, 3 * DA_WIDTH], axis=-1)
    q = _rope(q.reshape(b, s, 2 * DA_HEADS, DA_QK_DIM), positions)
    k = _rope(k.reshape(b, s, 2 * DA_HEADS, DA_QK_DIM), positions)
    q = q.reshape(b, s, DA_HEADS, 2, DA_QK_DIM)
    k = k.reshape(b, s, DA_HEADS, 2, DA_QK_DIM)
    v = v.reshape(b, s, DA_HEADS, DA_V_DIM)
    f32 = jnp.float32
    lam = (jnp.exp(jnp.sum(lam_q1.astype(f32) * lam_k1.astype(f32)))
           - jnp.exp(jnp.sum(lam_q2.astype(f32) * lam_k2.astype(f32))) + lambda_init)
    scale = DA_QK_DIM ** -0.5

    def block(qb):
        sc = jnp.einsum('bqhcd,bkhcd->bhcqk', qb, k, preferred_element_type=f32) * scale
        p = jax.nn.softmax(sc, axis=-1)
        p = p[:, :, 0] - lam * p[:, :, 1]
        return jnp.einsum('bhqk,bkhd->bqhd', p.astype(v.dtype), v)

    o = _from_blocks(lax.map(block, _to_blocks(q)))
    o = _rmsnorm(o, subln_g) * (1.0 - lambda_init)
    o = o.reshape(b, s, DA_WIDTH) * jax.nn.silu(z)
    return o @ w_out


def _mla(h, positions, w_in, q_a_norm_g, w_q_b, kv_a_norm_g, w_kv_b, w_out):
    b, s, _ = h.shape
    proj = h @ w_in
    q_a, c_kv, k_pe, z = jnp.split(
        proj, [MLA_Q_LORA, MLA_Q_LORA + MLA_KV_LORA, MLA_Q_LORA + MLA_KV_LORA + MLA_ROPE], axis=-1)
    q = (_rmsnorm(q_a, q_a_norm_g) @ w_q_b).reshape(b, s, MLA_HEADS, MLA_NOPE + MLA_ROPE)
    q_nope = q[..., :MLA_NOPE]
    q_pe = _rope(q[..., MLA_NOPE:], positions)
    k_pe = _rope(k_pe[:, :, None, :], positions)[:, :, 0]
    kv = (_rmsnorm(c_kv, kv_a_norm_g) @ w_kv_b).reshape(b, s, MLA_HEADS, MLA_NOPE + MLA_V)
    k_nope, v = kv[..., :MLA_NOPE], kv[..., MLA_NOPE:]
    scale = (MLA_NOPE + MLA_ROPE) ** -0.5
    f32 = jnp.float32

    def block(args):
        qn, qp = args
        sc = (jnp.einsum('bqhd,bkhd->bhqk', qn, k_nope, preferred_element_type=f32)
              + jnp.einsum('bqhr,bkr->bhqk', qp, k_pe, preferred_element_type=f32)) * scale
        p = jax.nn.softmax(sc, axis=-1)
        return jnp.einsum('bhqk,bkhd->bqhd', p.astype(v.dtype), v)

    o = _from_blocks(lax.map(block, (_to_blocks(q_nope), _to_blocks(q_pe))))
    o = o.reshape(b, s, MLA_WIDTH) * jax.nn.silu(z)
    return o @ w_out


def setup_inputs(seed: int = 0) -> dict:
    key = jax.random.key(seed)
    ks = jax.random.split(key, 24)
    f32 = jnp.float32
    nrm = lambda k, shape, fan_in: jax.random.normal(k, shape, f32) * (fan_in ** -0.5)
    gain = lambda k, n: 1.0 + 0.02 * jax.random.normal(k, (n,), f32)
    D = D_MODEL
    x = jax.random.normal(ks[0], (BATCH, SEQ, D), f32)
    c = jax.random.normal(ks[1], (BATCH, D), f32)
    offs = jax.random.randint(ks[2], (BATCH, 1), 0, MAX_POS_OFFSET, dtype=jnp.int32)
    positions = offs + jnp.arange(SEQ, dtype=jnp.int32)[None, :]
    return {
        "x": x,
        "c": c,
        "positions": positions,
        "ada_w0": nrm(ks[3], (D, 3 * D), D),
        "ada_b0": 0.02 * jax.random.normal(ks[4], (3 * D,), f32),
        "norm_g0": gain(ks[5], D),
        "w_in0": nrm(ks[6], (D, 4 * DA_WIDTH), D),
        "lam_q1": 0.1 * jax.random.normal(ks[7], (DA_QK_DIM,), f32),
        "lam_k1": 0.1 * jax.random.normal(ks[8], (DA_QK_DIM,), f32),
        "lam_q2": 0.1 * jax.random.normal(ks[9], (DA_QK_DIM,), f32),
        "lam_k2": 0.1 * jax.random.normal(ks[10], (DA_QK_DIM,), f32),
        "subln_g": gain(ks[11], DA_V_DIM),
        "w_out0": nrm(ks[12], (DA_WIDTH, D), DA_WIDTH),
        "ada_w1": nrm(ks[13], (D, 3 * D), D),
        "ada_b1": 0.02 * jax.random.normal(ks[14], (3 * D,), f32),
        "norm_g1": gain(ks[15], D),
        "w_in1": nrm(ks[16], (D, MLA_IN), D),
        "q_a_norm_g": gain(ks[17], MLA_Q_LORA),
        "w_q_b": nrm(ks[18], (MLA_Q_LORA, MLA_HEADS * (MLA_NOPE + MLA_ROPE)), MLA_Q_LORA),
        "kv_a_norm_g": gain(ks[19], MLA_KV_LORA),
        "w_kv_b": nrm(ks[20], (MLA_KV_LORA, MLA_HEADS * (MLA_NOPE + MLA_V)), MLA_KV_LORA),
        "w_out1": nrm(ks[21], (MLA_WIDTH, D), MLA_WIDTH),
        "final_norm_g": gain(ks[22], D),
    }


def reference(x, c, positions,
              ada_w0, ada_b0, norm_g0, w_in0, lam_q1, lam_k1, lam_q2, lam_k2, subln_g, w_out0,
              ada_w1, ada_b1, norm_g1, w_in1, q_a_norm_g, w_q_b, kv_a_norm_g, w_kv_b, w_out1,
              final_norm_g):
    ada_w = (ada_w0, ada_w1)
    ada_b = (ada_b0, ada_b1)
    norm_g = (norm_g0, norm_g1)
    for i in range(DEPTH):
        shift, scale, gate = _modulation(c, ada_w[i], ada_b[i])
        h = _rmsnorm(x, norm_g[i]) * (1.0 + scale) + shift
        if i % N_MIXERS == 0:
            y = _diff_attention(h, positions, w_in0, lam_q1, lam_k1, lam_q2, lam_k2,
                                subln_g, w_out0, _lambda_init(i))
        else:
            y = _mla(h, positions, w_in1, q_a_norm_g, w_q_b, kv_a_norm_g, w_kv_b, w_out1)
        x = x + gate * y
    return _rmsnorm(x, final_norm_g)
```

```python
import os
from contextlib import ExitStack

import numpy as np
import ml_dtypes

import concourse.bass as bass
import concourse.mybir as mybir
from concourse.bass_utils import run_bass_kernel_spmd

GROUPS = [[0, 1, 2, 3], [4, 5, 6, 7]]
F32 = mybir.dt.float32
BF16 = mybir.dt.bfloat16
I32 = mybir.dt.int32
AF = mybir.ActivationFunctionType
ALU = mybir.AluOpType
AX = mybir.AxisListType

D = 1024
S = 8192
TO = 2048
CH = 512
KC = 8
EPS = 1e-6
LAMBDA_INIT0 = 0.8 - 0.6 * 1.0
TWO_PI = 6.283185307179586
CW1 = 6.28125
CW2 = TWO_PI - CW1
HALF_PI = 1.5707963267948966


class Res:
    __slots__ = ("name", "w", "r")

    def __init__(self, name=""):
        self.name = name
        self.w = None
        self.r = {}


class Prog:
    ENGS = ("pe", "act", "dve", "pool", "sp")

    def __init__(self, nc, st, n_dma_sems=12):
        self.nc = nc
        self.q = {e: [] for e in self.ENGS}
        self.esem = {e: st.enter_context(nc.semaphore("s_" + e)) for e in self.ENGS}
        self.tick = {e: 0 for e in self.ENGS}
        self.pending = {e: False for e in self.ENGS}
        self.known = {e: {} for e in self.ENGS}
        self.dsem = {}
        for e in ("sp", "act", "pool"):
            self.dsem[e] = [[st.enter_context(nc.semaphore("d_%s%d" % (e, i))), 0] for i in range(n_dma_sems)]
        self.drr = {e: 0 for e in self.dsem}
        self.ccsem = st.enter_context(nc.semaphore("cc_done"))
        self.ccval = 0

    def _waits(self, e, reads, writes):
        my = self.esem[e]
        need = {}

        def add(sv, same_ok):
            if sv is None:
                return
            s, v = sv
            if same_ok and s is my:
                return
            k = id(s)
            if k not in need or need[k][1] < v:
                need[k] = (s, v)

        for r in reads:
            add(r.w, False)
        for w in writes:
            add(w.w, True)
            for sv in w.r.values():
                add(sv, True)
        out = []
        kn = self.known[e]
        for k, (s, v) in need.items():
            if kn.get(k, 0) >= v:
                continue
            kn[k] = v
            out.append((s, v))
        return out

    @staticmethod
    def _mark(reads, writes, s, v):
        for r in reads:
            old = r.r.get(id(s))
            if old is None or old[1] < v:
                r.r[id(s)] = (s, v)
        for w in writes:
            w.w = (s, v)
            w.r = {}

    def op(self, e, fn, reads=(), writes=(), signal=True):
        waits = self._waits(e, reads, writes)
        s = self.esem[e]
        if signal:
            self.tick[e] += 1
            v = self.tick[e]
            self.pending[e] = False
            inc = (s, 1)
        else:
            v = self.tick[e] + 1
            self.pending[e] = True
            inc = None
        self._mark(reads, writes, s, v)
        self.q[e].append((waits, fn, inc))

    def dma(self, e, out, in_, reads=(), writes=(), **kw):
        lst = self.dsem[e]
        i = self.drr[e]
        self.drr[e] = (i + 1) % len(lst)
        ent = lst[i]
        s = ent[0]
        waits = self._waits(e, reads, writes)
        prev = ent[1]
        kn = self.known[e]
        if prev > 0 and kn.get(id(s), 0) < prev:
            kn[id(s)] = prev
            waits.append((s, prev))
        ent[1] = prev + 16
        v = ent[1]
        self._mark(reads, writes, s, v)
        self.q[e].append((waits, lambda eng: eng.dma_start(out=out, in_=in_, **kw), (s, 16)))

    def cc(self, fn, reads=(), writes=()):
        waits = self._waits("pool", reads, writes)
        inc = 1
        self.ccval += inc
        self._mark(reads, writes, self.ccsem, self.ccval)
        self.q["pool"].append((waits, fn, (self.ccsem, inc)))

    def barrier(self, skip=()):
        pts = []
        for e in self.ENGS:
            assert not self.pending[e]
            if self.tick[e] > 0 and e not in skip:
                pts.append((self.esem[e], self.tick[e]))
        for e in self.dsem:
            for s, v in self.dsem[e]:
                if v > 0:
                    pts.append((s, v))
        if self.ccval > 0 and "pool" not in skip:
            pts.append((self.ccsem, self.ccval))
        for e in self.ENGS:
            kn = self.known[e]
            waits = []
            for s, v in pts:
                if s is self.esem[e]:
                    continue
                if kn.get(id(s), 0) < v:
                    kn[id(s)] = v
                    waits.append((s, v))
            if waits:
                self.q[e].append((waits, None, None))

    def emit(self):
        nc = self.nc
        for e in self.ENGS:
            assert not self.pending[e], e
        with nc.Block() as block:
            def run(eng, lst):
                for waits, fn, inc in lst:
                    for s, v in waits:
                        eng.wait_ge(s, v)
                    if fn is not None:
                        ins = fn(eng)
                        if inc is not None:
                            ins.then_inc(inc[0], inc[1])

            @block.tensor
            def _(eng):
                run(eng, self.q["pe"])

            @block.scalar
            def _(eng):
                run(eng, self.q["act"])

            @block.vector
            def _(eng):
                run(eng, self.q["dve"])

            @block.gpsimd
            def _(eng):
                run(eng, self.q["pool"])

            @block.sync
            def _(eng):
                run(eng, self.q["sp"])


class Arena:
    def __init__(self, nc, st, nbytes):
        self.t = st.enter_context(nc.sbuf_tensor("arena", [128, nbytes // 2], BF16))
        self.n = nbytes
        self.off = 0

    def alloc(self, free_shape, dt, parts=128):
        esz = 4 if dt in (F32, I32) else 2
        n = 1
        for d in free_shape:
            n *= d
        nb = (n * esz + 63) // 64 * 64
        assert self.off + nb <= self.n, ("arena overflow", self.off, nb, self.n)
        ap = self.t[:, self.off // 2:(self.off + n * esz) // 2]
        self.off += nb
        self.peak = max(getattr(self, "peak", 0), self.off)
        if dt != BF16:
            ap = ap.bitcast(dt)
        if len(free_shape) == 2:
            ap = ap.rearrange("p (a b) -> p a b", b=free_shape[1])
        elif len(free_shape) == 3:
            ap = ap.rearrange("p (a b c) -> p a b c", b=free_shape[1], c=free_shape[2])
        if parts != 128:
            ap = ap[0:parts]
        return ap

    def mark(self):
        return self.off

    def release(self, m):
        self.off = m


class K:
    def __init__(self, name):
        self.ext = {}
        self.gres = {}
        self.fused = False
        self.consts_loaded = False
        self.tables_done = False
        self.nc = bass.Bass("TRN2", target_bir_lowering=False)
        self.st = ExitStack()
        self.P = Prog(self.nc, self.st)
        self.A = Arena(self.nc, self.st, 200 * 1024)
        self.ps = self.st.enter_context(self.nc.psum_tensor("ps", [128, 4096], F32))
        self.rb = [Res("bank%d" % i) for i in range(8)]
        self.cres = Res("consts")

    def bank(self, i, n=512):
        return self.ps[:, i * 512:i * 512 + n]

    def bank_bf(self, i):
        return self.ps[:, i * 512:(i + 1) * 512].bitcast(BF16)

    def din(self, name, shape, dt=F32):
        if name in self.ext:
            return self.ext[name]
        return self.nc.dram_tensor(name, list(shape), dt, kind="ExternalInput").ap()

    def dout(self, name, shape, dt=F32):
        if name in self.ext:
            return self.ext[name]
        return self.nc.dram_tensor(name, list(shape), dt, kind="ExternalOutput").ap()

    def dint(self, name, shape, dt=F32):
        if name in self.ext:
            return self.ext[name]
        return self.nc.dram_tensor(name, list(shape), dt).ap()

    def done(self):
        if not self.fused:
            return self.finish()
        self.P.barrier(skip=("pool",))
        if os.environ.get("KDBG"):
            print("arena peak", self.A.peak, "tick", dict(self.P.tick))
        self.A.peak = 0
        self.A.release(self.base_mark)
        return None

    def gather_piece(self, name, idx, src, dst, reads):
        self.P.cc(lambda e: e.collective_compute("AllGather", ALU.bypass, replica_groups=GROUPS, ins=[src.opt()], outs=[dst.opt()]),
                  reads, [self.gres[name][idx]])

    def finish(self):
        self.P.barrier()
        self.P.emit()
        self.st.close()
        return self.nc

    def mm(self, out, lhsT, rhs, start, stop, reads, writes, signal=True):
        self.P.op("pe", lambda e: e.matmul(out, lhsT=lhsT, rhs=rhs, start=start, stop=stop), reads, writes, signal)

    def act(self, out, in_, func, reads, writes, **kw):
        self.P.op("act", lambda e: e.activation(out=out, in_=in_, func=func, **kw), reads, writes)

    def copy(self, eng, out, in_, reads, writes):
        if eng == "act":
            self.P.op(eng, lambda e: e.activation(out=out, in_=in_, func=AF.Copy), reads, writes)
        else:
            self.P.op(eng, lambda e: e.tensor_copy(out=out, in_=in_), reads, writes)

    def tt(self, eng, out, in0, in1, op, reads, writes):
        self.P.op(eng, lambda e: e.tensor_tensor(out=out, in0=in0, in1=in1, op=op), reads, writes)

    def ts(self, eng, out, in0, s1, s2, op0, op1, reads, writes):
        if s2 is None:
            self.P.op(eng, lambda e: e.tensor_scalar(out=out, in0=in0, scalar1=s1, scalar2=None, op0=op0), reads, writes)
        else:
            self.P.op(eng, lambda e: e.tensor_scalar(out=out, in0=in0, scalar1=s1, scalar2=s2, op0=op0, op1=op1), reads, writes)

    def stt(self, eng, out, in0, scalar, in1, op0, op1, reads, writes):
        self.P.op(eng, lambda e: e.scalar_tensor_tensor(out=out, in0=in0, scalar=scalar, in1=in1, op0=op0, op1=op1), reads, writes)

    def warm(self, n, bank=7):
        for i in range(n):
            self.mm(self.bank(bank), self.ones_b, self.warm_rhs, True, True, [self.cres], [self.rb[bank]], signal=(i == n - 1))

    def recip(self, out, in_, reads, writes):
        self.P.op("dve", lambda e: e.reciprocal(out=out, in_=in_), reads, writes)

    def rsqrt_inplace(self, ap, res, mul, add):
        self.ts("dve", ap, ap, mul, add, ALU.mult, ALU.add, [res], [res])
        self.act(ap, ap, AF.Sqrt, [res], [res])
        self.recip(ap, ap, [res], [res])

    def load_consts(self, ident_d, need_ident=True):
        A, P = self.A, self.P
        if self.consts_loaded:
            return
        self.consts_loaded = True
        self.ones_b = A.alloc([128], BF16)
        self.ones_f = A.alloc([128], F32)
        P.op("pool", lambda e: e.memset(self.ones_b, 1.0), [], [self.cres])
        P.op("pool", lambda e: e.memset(self.ones_f, 1.0), [], [self.cres])
        self.warm_rhs = A.alloc([512], BF16)
        P.op("pool", lambda e: e.memset(self.warm_rhs, 1.0), [], [self.cres])
        if need_ident:
            idf = A.alloc([128], F32)
            self.ident_b = A.alloc([128], BF16)
            r = Res()
            P.dma("sp", idf, ident_d[:, :], [], [r])
            self.copy("dve", self.ident_b, idf, [r], [self.cres])


def w_view(w, c0, c1):
    return w[:, c0:c1].rearrange("(kc p) n -> p kc n", p=128)


def phase_mod(k, cvec_d, ada_w, ada_b, norm_g, modrow_d, mod_res, sync=True):
    A, P = k.A, k.P
    m = A.mark()
    cv = A.alloc([KC], F32)
    sc = A.alloc([KC], F32)
    wb = [A.alloc([KC, 512], F32) for _ in range(2)]
    row = A.alloc([3072], F32, parts=1)
    brow = A.alloc([3072], F32, parts=1)
    grow = A.alloc([1024], F32, parts=1)
    rcv, rsc, rrow, rbrow, rgrow = Res(), Res(), Res(), Res(), Res()
    rwb = [Res(), Res()]
    P.dma("sp", cv, cvec_d[:, :], [], [rcv])
    P.dma("sp", brow, ada_b[:, :], [], [rbrow])
    P.dma("sp", grow, norm_g[:, :], [], [rgrow])
    k.act(sc, cv, AF.Silu, [rcv], [rsc])
    for n in range(6):
        i = n % 2
        P.dma("sp" if n % 2 == 0 else "pool", wb[i], w_view(ada_w, n * 512, (n + 1) * 512), [], [rwb[i]])
        bk = n % 2
        for kc in range(KC):
            k.mm(k.bank(bk)[0:1, :], sc[:, kc:kc + 1], wb[i][:, kc, :], kc == 0, kc == KC - 1,
                 [rsc, rwb[i]], [k.rb[bk]], signal=(kc == KC - 1))
        k.tt("dve", row[:, n * 512:(n + 1) * 512], k.bank(bk)[0:1, :], brow[:, n * 512:(n + 1) * 512], ALU.add,
             [k.rb[bk], rbrow], [rrow])
    k.stt("dve", row[:, 1024:2048], row[:, 1024:2048], 1.0, grow, ALU.add, ALU.mult, [rrow, rgrow], [rrow])
    P.dma("sp", modrow_d[:, :], row, [rrow], [mod_res])
    if sync:
        k.P.barrier()
        A.release(m)


def phase_tables(k, pos_d, cst, rcst, cos_d, sin_d, ntok, tab_res, sync=True):
    for _ in tables_gen(k, pos_d, cst, rcst, cos_d, sin_d, ntok, tab_res, sync):
        pass


def tables_gen(k, pos_d, cst, rcst, cos_d, sin_d, ntok, tab_res, sync=True, bufs=None):
    A, P = k.A, k.P
    m = A.mark()
    C = 2048
    posi = A.alloc([C], I32)
    a = A.alloc([C], F32)
    kf = A.alloc([C], F32)
    ki = A.alloc([C], I32)
    so = A.alloc([C], F32)
    co = A.alloc([C], F32)
    rp, ra, rk, rki, rso, rco = Res(), Res(), Res(), Res(), Res(), Res()
    for c in range(ntok // C):
        P.dma("sp", posi, pos_d[0:1, c * C:(c + 1) * C].partition_broadcast(128), [], [rp])
        k.copy("dve", a, posi, [rp], [ra])
        k.ts("dve", a, a, cst[:, 0:1], None, ALU.mult, None, [ra, rcst], [ra])
        k.ts("dve", kf, a, 1.0 / TWO_PI, None, ALU.mult, None, [ra], [rk])
        k.copy("dve", ki, kf, [rk], [rki])
        k.copy("dve", kf, ki, [rki], [rk])
        k.stt("dve", a, kf, -CW1, a, ALU.mult, ALU.add, [rk, ra], [ra])
        k.stt("dve", a, kf, -CW2, a, ALU.mult, ALU.add, [rk, ra], [ra])
        k.ts("dve", a, a, 3.1415925, -3.1415925, ALU.min, ALU.max, [ra], [ra])
        k.act(so, a, AF.Sin, [ra], [rso])
        k.ts("dve", kf, a, HALF_PI, -TWO_PI, ALU.is_gt, ALU.mult, [ra], [rk])
        k.stt("dve", kf, a, HALF_PI, kf, ALU.add, ALU.add, [ra, rk], [rk])
        k.act(co, kf, AF.Sin, [rk], [rco])
        P.dma("sp", sin_d[:, c * C:(c + 1) * C], so, [rso], [tab_res])
        P.dma("act", cos_d[:, c * C:(c + 1) * C], co, [rco], [tab_res])
        yield c
    if sync:
        k.P.barrier()
        A.release(m)


class PreNorm:
    def __init__(self, k, modrow_d, mod_res):
        A, P = k.A, k.P
        self.k = k
        self.gmod = A.alloc([D], F32)
        self.shift = A.alloc([D], F32)
        self.rmod = Res()
        self.rshift = Res()
        P.dma("sp", self.shift, modrow_d[0:1, 0:D].partition_broadcast(128), [mod_res], [self.rshift])
        P.dma("pool", self.gmod, modrow_d[0:1, D:2 * D].partition_broadcast(128), [mod_res], [self.rmod])
        self.junk = A.alloc([D], F32)
        self.NB = 3
        self.tmp = [A.alloc([D], F32) for _ in range(self.NB)]
        self.rtmp = [Res() for _ in range(self.NB)]
        self.ss = A.alloc([16 * 16], F32)
        self.rss = [Res() for _ in range(16)]
        P.op("dve", lambda e: e.memset(self.ss, 0.0), [], self.rss)
        self.hb = [A.alloc([D], BF16) for _ in range(self.NB)]
        self.rhb = [Res() for _ in range(self.NB)]
        self.tpb = (6, 7)

    def s0(self, tt, xt, rx):
        self.k.act(self.junk, xt, AF.Square, [rx], [self.rss[tt]], accum_out=self.ss[:, tt * 16:tt * 16 + 1])

    def s1(self, tt):
        self.k.rsqrt_inplace(self.ss[:, tt * 16:tt * 16 + 1], self.rss[tt], 1.0 / D, EPS)

    def s2(self, tt, xt, rx, hT_dst, r_hT):
        k = self.k
        i = tt % self.NB
        ssc = self.ss[:, tt * 16:tt * 16 + 1]
        k.stt("dve", self.tmp[i], xt, ssc, self.gmod, ALU.mult, ALU.mult, [rx, self.rss[tt], self.rmod], [self.rtmp[i]])
        k.tt("dve", self.hb[i], self.tmp[i], self.shift, ALU.add, [self.rtmp[i], self.rshift], [self.rhb[i]])
        bk = self.tpb[tt % 2]
        tp = k.bank_bf(bk)
        for kc in range(KC):
            k.P.op("pe", (lambda kc=kc: (lambda e: e.transpose(tp[:, kc * 128:(kc + 1) * 128], self.hb[i][:, kc * 128:(kc + 1) * 128], k.ident_b)))(),
                   [self.rhb[i], k.cres], [k.rb[bk]], signal=(kc == KC - 1))
        k.copy("act", hT_dst, tp.rearrange("p (kc t) -> p kc t", t=128), [k.rb[bk]], [r_hT])


def run_pipeline(n, stages):
    ns = len(stages)
    for step in range(n + ns - 1):
        for s in range(ns):
            t = step - s
            if 0 <= t < n:
                stages[s](t)


def build_L1(k=None):
    k = k or K("L1")
    x = k.din("x", [TO, D])
    cvec = k.din("cvec", [128, KC])
    ada_w = k.din("ada_w", [D, 3 * D])
    ada_b = k.din("ada_b", [1, 3 * D])
    norm_g = k.din("norm_g", [1, D])
    ident = k.din("ident", [128, 128])
    hT_o = k.dout("hT", [D, TO], BF16)
    hT_pc = (lambda c: k.ext["hT_c"][c]) if k.fused else (lambda c: hT_o[:, c * CH:(c + 1) * CH])
    modrow = k.dout("modrow", [1, 3 * D])
    k.load_consts(ident)
    rmod = Res()
    phase_mod(k, cvec, ada_w, ada_b, norm_g, modrow, rmod)
    A, P = k.A, k.P
    pn = PreNorm(k, modrow, rmod)
    xb = [A.alloc([D], F32) for _ in range(4)]
    rxb = [Res() for _ in range(4)]
    hTs = [A.alloc([KC, CH], BF16) for _ in range(2)]
    rhT = [Res(), Res()]
    gens = []
    mtab = A.mark()
    if k.fused:
        X = k.ext["early"]
        cst = A.alloc([8], F32)
        rcst = Res()
        P.dma("sp", cst, X["cst"][:, :], [], [rcst])
        gens = [tables_gen(k, X["pos"], cst, rcst, X["cosT"], X["sinT"], S, Res(), sync=False),
                tables_gen(k, X["pos_own"], cst, rcst, X["cosO"], X["sinO"], TO, Res(), sync=False)]
        k.tables_done = True
    def l1_s0(tt):
        i = tt % 4
        P.dma("sp" if i % 2 == 0 else "act", xb[i], x[tt * 128:(tt + 1) * 128, :], [], [rxb[i]])
        if tt % 3 == 1 and gens:
            if next(gens[0], None) is None:
                gens.pop(0)
        pn.s0(tt, xb[i], rxb[i])

    def l1_s2(tt):
        i = tt % 4
        c = tt // 4
        j = c % 2
        pn.s2(tt, xb[i], rxb[i], hTs[j][:, :, (tt % 4) * 128:(tt % 4 + 1) * 128], rhT[j])
        if tt % 4 == 3:
            rpc = Res()
            P.dma("sp", hT_pc(c).rearrange("(kc p) t -> p kc t", p=128), hTs[j], [rhT[j]], [rpc])
            if k.fused:
                k.gather_piece("HT", c, k.ext["hT_c"][c], k.ext["HT_dst"][c], [rpc])

    run_pipeline(16, [l1_s0, pn.s1, l1_s2])
    for g in gens:
        for _ in g:
            pass
    if k.fused:
        P.barrier(skip=("pool",))
        A.release(mtab)
        X = k.ext["early"]
        phase_mod(k, cvec, X["ada_w1"], X["ada_b1"], X["norm_g1"], X["modrow1"], Res(), sync=False)
    return k.done()


def attention_block(k, n_units, qk_fn, pv_fn, scale, PT, rPT, sbank0=0, warm0=0, warm_bank=7):
    def exp_unit(u):
        b = sbank0 + 2 * (u % 2)
        i = u % len(PT)
        k.act(PT[i], k.ps[:, b * 512:(b + 2) * 512], AF.Exp, [k.rb[b], k.rb[b + 1]], [rPT[i]], scale=scale)

    qk_fn(0, sbank0)
    if n_units > 1:
        qk_fn(1, sbank0 + 2)
    for u in range(n_units):
        exp_unit(u)
        if u == 0 and warm0:
            k.warm(warm0, warm_bank)
        if u + 2 < n_units:
            qk_fn(u + 2, sbank0 + 2 * (u % 2))
        pv_fn(u, PT[u % len(PT)], rPT[u % len(PT)])


def build_L2(k=None):
    k = k or K("L2")
    HT = k.din("HT", [4 * D, TO], BF16)
    pos = k.din("pos", [1, S], I32)
    cst_d = k.din("cst", [128, 8])
    wq_d = k.din("wq", [D, 256])
    wk_d = k.din("wk", [D, 256])
    wv_d = k.din("wv", [D, 256])
    lamv_d = k.din("lamv", [1, 256])
    ident = k.din("ident", [128, 128])
    oT_o = k.dout("oT", [256, S], BF16)
    if k.fused:
        HT_pc = lambda r, lc: k.ext["HT_c"][lc][r * D:(r + 1) * D, :]
        oT_pc = lambda h, qb: k.ext["oT_c"][qb // 4][h * 128:(h + 1) * 128, (qb % 4) * CH:(qb % 4 + 1) * CH]
    else:
        HT_pc = lambda r, lc: HT[r * D:(r + 1) * D, lc * CH:(lc + 1) * CH]
        oT_pc = lambda h, qb: oT_o[h * 128:(h + 1) * 128, qb * CH:(qb + 1) * CH]
    cos_d = k.dint("cosT", [128, S])
    sin_d = k.dint("sinT", [128, S])
    A, P = k.A, k.P
    k.load_consts(ident, need_ident=False)
    cst = A.alloc([8], F32)
    rcst = Res()
    P.dma("sp", cst, cst_d[:, :], [], [rcst])
    rtab = Res()
    if k.fused:
        pass
    elif not k.tables_done:
        phase_tables(k, pos, cst, rcst, cos_d, sin_d, S, rtab)

    lamv = A.alloc([256], F32, parts=1)
    lt = A.alloc([128], F32, parts=1)
    l2 = A.alloc([2], F32, parts=1)
    nl = A.alloc([1], F32, parts=1)
    neglam = A.alloc([1], F32)
    rl, rlt, rl2, rnl, rneg = Res(), Res(), Res(), Res(), Res()
    P.dma("sp", lamv, lamv_d[:, :], [], [rl])
    k.tt("dve", lt[:, 0:64], lamv[:, 0:64], lamv[:, 64:128], ALU.mult, [rl], [rlt])
    k.tt("dve", lt[:, 64:128], lamv[:, 128:192], lamv[:, 192:256], ALU.mult, [rl], [rlt])
    P.op("dve", lambda e: e.reduce_sum(out=l2[:, 0:1], in_=lt[:, 0:64], axis=AX.X), [rlt], [rl2])
    P.op("dve", lambda e: e.reduce_sum(out=l2[:, 1:2], in_=lt[:, 64:128], axis=AX.X), [rlt], [rl2])
    k.act(l2, l2, AF.Exp, [rl2], [rl2])
    k.tt("dve", nl, l2[:, 1:2], l2[:, 0:1], ALU.subtract, [rl2], [rnl])
    k.ts("dve", nl, nl, -LAMBDA_INIT0, None, ALU.add, None, [rnl], [rnl])
    k.mm(k.bank(0)[:, 0:1], k.ones_f[0:1, :], nl[0:1, 0:1], True, True, [k.cres, rnl], [k.rb[0]])
    k.copy("dve", neglam, k.bank(0)[:, 0:1], [k.rb[0]], [rneg])

    Wq = A.alloc([KC, 256], BF16)
    Wqr = A.alloc([KC, 256], BF16)
    Wk = A.alloc([KC, 256], BF16)
    Wkr = A.alloc([KC, 256], BF16)
    Wv = A.alloc([KC, 256], BF16)
    rW = Res()
    m = A.mark()
    stg = [A.alloc([KC, 256], F32) for _ in range(3)]
    rstg = [Res(), Res(), Res()]
    for i, (wd, Wn, Wr) in enumerate(((wq_d, Wq, Wqr), (wk_d, Wk, Wkr), (wv_d, Wv, None))):
        P.dma("sp", stg[i], w_view(wd, 0, 256), [], [rstg[i]])
        k.copy("dve", Wn, stg[i], [rstg[i]], [rW])
        if Wr is not None:
            sv = stg[i].rearrange("p kc (b two h) -> p (kc b) two h", two=2, h=32)
            dv = Wr.rearrange("p kc (b two h) -> p (kc b) two h", two=2, h=32)
            k.ts("dve", dv[:, :, 0, :], sv[:, :, 1, :], -1.0, None, ALU.mult, None, [rstg[i]], [rW])
            k.copy("dve", dv[:, :, 1, :], sv[:, :, 0, :], [rstg[i]], [rW])
    P.barrier()
    A.release(m)

    KT = A.alloc([2, S], BF16)
    V = A.alloc([64, 256], BF16)
    rKT, rV = Res(), Res()
    hb = [A.alloc([KC, CH], BF16) for _ in range(2)]
    rhb = [Res(), Res()]
    cb = [A.alloc([CH], F32) for _ in range(2)]
    sb_ = [A.alloc([CH], F32) for _ in range(2)]
    rtb = [Res(), Res()]
    rtb2 = [Res(), Res()]
    t1 = [A.alloc([CH], F32) for _ in range(2)]
    t2 = [A.alloc([CH], F32) for _ in range(2)]
    rt1 = [Res(), Res()]
    rt2 = [Res(), Res()]

    def load_chunk(ck, i):
        r, lc = ck // 4, ck % 4
        P.dma("sp", hb[i], HT_pc(r, lc).rearrange("(kc p) t -> p kc t", p=128), [k.gres["HT"][lc]] if k.fused else [], [rhb[i]])
        P.dma("act", cb[i], cos_d[:, ck * CH:(ck + 1) * CH], [rtab], [rtb[i]])
        P.dma("act", sb_[i], sin_d[:, ck * CH:(ck + 1) * CH], [rtab], [rtb2[i]])

    def proj_rope(i, W, Wr, h, b0, dst, rdst, j):
        for kc in range(KC):
            k.mm(k.bank(b0), W[:, kc, h * 128:(h + 1) * 128], hb[i][:, kc, :], kc == 0, kc == KC - 1,
                 [rW, rhb[i]], [k.rb[b0]], signal=(kc == KC - 1))
        for kc in range(KC):
            k.mm(k.bank(b0 + 1), Wr[:, kc, h * 128:(h + 1) * 128], hb[i][:, kc, :], kc == 0, kc == KC - 1,
                 [rW, rhb[i]], [k.rb[b0 + 1]], signal=(kc == KC - 1))
        k.tt("dve", t1[j], k.bank(b0), cb[i], ALU.mult, [k.rb[b0], rtb[i]], [rt1[j]])
        k.tt("dve", t2[j], k.bank(b0 + 1), sb_[i], ALU.mult, [k.rb[b0 + 1], rtb2[i]], [rt2[j]])
        k.tt("dve", dst, t1[j], t2[j], ALU.add, [rt1[j], rt2[j]], [rdst])

    order = [r * 4 + lc for lc in range(4) for r in range(4)] if os.environ.get("KORD", "1") == "1" else list(range(16))
    load_chunk(order[0], 0)
    for n_ in range(16):
        ck = order[n_]
        i = n_ % 2
        if n_ + 1 < 16:
            load_chunk(order[n_ + 1], 1 - i)
        for h in range(2):
            proj_rope(i, Wk, Wkr, h, 2 * h, KT[:, h, ck * CH:(ck + 1) * CH], rKT, h)
        for ts_ in range(4):
            bk = 4 + ts_ % 2
            for kc in range(KC):
                k.mm(k.bank(bk, 256), hb[i][:, kc, ts_ * 128:(ts_ + 1) * 128], Wv[:, kc, :], kc == 0, kc == KC - 1,
                     [rW, rhb[i]], [k.rb[bk]], signal=(kc == KC - 1))
            k.copy("act", V[:, ck * 4 + ts_, :], k.bank(bk, 256), [k.rb[bk]], [rV])

    QT = [A.alloc([2, CH], BF16) for _ in range(2)]
    rQT = [[Res(), Res()], [Res(), Res()]]
    PT = [A.alloc([2 * CH], BF16) for _ in range(4)]
    rPT = [Res() for _ in range(4)]
    acc = [A.alloc([2 * CH], F32) for _ in range(2)]
    racc = [Res(), Res()]
    r1 = A.alloc([CH], F32)
    r2 = A.alloc([CH], F32)
    o1 = A.alloc([CH], F32)
    o2 = A.alloc([CH], F32)
    rr1, rr2, ro1, ro2 = Res(), Res(), Res(), Res()
    ob = [A.alloc([CH], BF16) for _ in range(2)]
    rob = [Res(), Res()]
    scale = 64 ** -0.5
    load_chunk(0, 0)
    cnt = 0
    piece_res = []
    wpre = WPrefetch(k, k.ext["wz_next"], k.ext["wo_next"]) if (k.fused and os.environ.get("KPRE", "1") == "1") else None
    for qb in range(16):
        i = qb % 2
        if qb + 1 < 16:
            load_chunk(qb + 1, 1 - i)
        if wpre is not None and qb >= 8:
            wpre.step()
        for h in range(2):
            proj_rope(i, Wq, Wqr, h, 2 * h, QT[i][:, h, :], rQT[i][h], h)
        for h in range(2):
            q_ap = QT[i]
            rq = rQT[i][h]

            def qk_fn(u, b0, h=h, q_ap=q_ap, rq=rq):
                k.mm(k.bank(b0), KT[0:64, h, u * 128:(u + 1) * 128], q_ap[0:64, h, :], True, True, [rKT, rq], [k.rb[b0]])
                k.mm(k.bank(b0 + 1), KT[64:128, h, u * 128:(u + 1) * 128], q_ap[64:128, h, :], True, True, [rKT, rq], [k.rb[b0 + 1]])

            aset = cnt % 2

            def pv_fn(u, pt, rpt, h=h, aset=aset):
                st, sp = (u == 0), (u == 63)
                k.mm(k.bank(4), V[:, u, h * 128:(h + 1) * 128], pt[:, 0:CH], st, sp, [rV, rpt], [k.rb[4]], signal=False)
                k.mm(k.bank(6), V[:, u, h * 128:(h + 1) * 128], pt[:, CH:2 * CH], st, sp, [rV, rpt], [k.rb[6]], signal=True)
                if st:
                    k.copy("dve", acc[aset], pt, [rpt], [racc[aset]])
                elif u % 2 == 1:
                    k.mm(k.bank(5), k.ones_b, pt[:, 0:CH], u == 1, False, [k.cres, rpt], [k.rb[5]])
                    k.tt("dve", acc[aset][:, CH:2 * CH], acc[aset][:, CH:2 * CH], pt[:, CH:2 * CH], ALU.add, [rpt, racc[aset]], [racc[aset]])
                else:
                    k.tt("dve", acc[aset], acc[aset], pt, ALU.add, [rpt, racc[aset]], [racc[aset]])

            attention_block(k, 64, qk_fn, pv_fn, scale, PT, rPT, warm0=int(os.environ.get("KWARM2", "14")), warm_bank=0)
            k.mm(k.bank(5), k.ones_f, acc[aset][:, 0:CH], False, True, [k.cres, racc[aset]], [k.rb[5]])
            k.mm(k.bank(7), k.ones_f, acc[aset][:, CH:2 * CH], True, True, [k.cres, racc[aset]], [k.rb[7]])
            k.recip(r1, k.bank(5), [k.rb[5]], [rr1])
            k.recip(r2, k.bank(7), [k.rb[7]], [rr2])
            k.tt("dve", o1, k.bank(4), r1, ALU.mult, [k.rb[4], rr1], [ro1])
            k.tt("dve", o2, k.bank(6), r2, ALU.mult, [k.rb[6], rr2], [ro2])
            j = cnt % 2
            cnt += 1
            k.stt("dve", ob[j], o2, neglam, o1, ALU.mult, ALU.add, [ro2, ro1, rneg], [rob[j]])
            rpc = Res()
            P.dma("sp", oT_pc(h, qb), ob[j], [rob[j]], [rpc])
            piece_res.append(rpc)
        if k.fused and qb % 4 == 3:
            k.gather_piece("OT", qb // 4, k.ext["oT_c"][qb // 4], k.ext["OT_dst"][qb // 4], piece_res)
            piece_res = []
    return k.done()


class WPrefetch:
    def __init__(self, k, wz_d, wo_d):
        self.k = k
        self.stg = [k.A.alloc([KC, 256], F32) for _ in range(2)]
        self.rstg = [Res(), Res()]
        self.todo = [(wd, Wn, c) for wd, Wn in ((wz_d, k.Wz_pre), (wo_d, k.Wo_pre)) for c in range(4)]
        self.n = 0

    def step(self):
        if not self.todo:
            return
        wd, Wn, c = self.todo.pop(0)
        i = self.n % 2
        self.n += 1
        k = self.k
        k.P.dma("sp" if i == 0 else "act", self.stg[i], w_view(wd, c * 256, (c + 1) * 256), [], [self.rstg[i]])
        k.copy("act", Wn[:, :, c * 256:(c + 1) * 256], self.stg[i], [self.rstg[i]], [k.rWpre])


def phase_C(k, layer, x_d, oT_d, hT_d, wz_d, wo_d, modrow_d, rmod, subln_d, tile_cb):
    A, P = k.A, k.P
    if k.fused and os.environ.get("KPRE", "1") == "1":
        Wz, Wo, rW = k.Wz_pre, k.Wo_pre, k.rWpre
    else:
        Wz = A.alloc([KC, D], BF16)
        Wo = A.alloc([KC, D], BF16)
        rW = Res()
        m = A.mark()
        stg = [A.alloc([KC, 256], F32) for _ in range(2)]
        rstg = [Res(), Res()]
        n = 0
        for wd, Wn in ((wz_d, Wz), (wo_d, Wo)):
            for c in range(4):
                i = n % 2
                n += 1
                P.dma("sp" if i == 0 else "pool", stg[i], w_view(wd, c * 256, (c + 1) * 256), [], [rstg[i]])
                k.copy("dve" if i == 0 else "act", Wn[:, :, c * 256:(c + 1) * 256], stg[i], [rstg[i]], [rW])
        P.barrier()
        A.release(m)
    gate = A.alloc([D], F32)
    rgate = Res()
    P.dma("sp", gate, modrow_d[0:1, 2 * D:3 * D].partition_broadcast(128), [rmod], [rgate])
    if layer == 0:
        gs = A.alloc([1], F32)
        rgs = Res()
        P.dma("sp", gs, subln_d[:, :], [], [rgs])
        k.ts("dve", gs, gs, 1.0 - LAMBDA_INIT0, None, ALU.mult, None, [rgs], [rgs])
        on = A.alloc([KC, CH], F32)
        ron = [Res() for _ in range(KC)]
        sq = [A.alloc([CH], F32) for _ in range(2)]
        rsq = [Res(), Res()]
    hb = [A.alloc([KC, CH], BF16) for _ in range(2)]
    rhb = [Res(), Res()]
    ob = [A.alloc([KC, CH], BF16) for _ in range(2)]
    rob = [Res(), Res()]
    sz = [A.alloc([CH], F32) for _ in range(2)]
    rsz = [Res(), Res()]
    og = [A.alloc([KC, CH], BF16) for _ in range(2)]
    rog = [Res(), Res()]
    xt = [A.alloc([D], F32) for _ in range(2)]
    rxt = [Res(), Res()]
    tmp = A.alloc([D], F32)
    rtmp = Res()
    xn = [A.alloc([D], F32) for _ in range(3)]
    rxn = [Res() for _ in range(3)]
    tail = []

    if k.fused:
        sel = A.alloc([4], F32)
        rsel = Res()
        P.dma("sp", sel, k.ext["sel"][:, :], [], [rsel])
        cand = [A.alloc([4, CH], BF16) for _ in range(4)]
        rcand = [Res() for _ in range(4)]

    def load_chunk(lc, i):
        if k.fused:
            P.dma("sp", hb[i], hT_d[lc].rearrange("(kc p) t -> p kc t", p=128), [], [rhb[i]])
        else:
            P.dma("sp", hb[i], hT_d[:, lc * CH:(lc + 1) * CH].rearrange("(kc p) t -> p kc t", p=128), [], [rhb[i]])
        if not k.fused:
            P.dma("pool", ob[i], oT_d[:, lc * CH:(lc + 1) * CH].rearrange("(kc p) t -> p kc t", p=128), [], [rob[i]])
            return
        for half in range(2):
            dst = ob[i][:, half * 4:(half + 1) * 4, :]
            for jj in range(4):
                P.dma("pool" if jj % 2 == 0 else "sp", cand[jj],
                      oT_d[jj][:, lc * CH:(lc + 1) * CH].rearrange("(kc p) t -> p kc t", p=128)[:, half * 4:(half + 1) * 4, :], [k.gres["OT"][jj]], [rcand[jj]])
            k.ts("dve", dst, cand[0], sel[:, 0:1], None, ALU.mult, None, [rcand[0], rsel], [rob[i]])
            for jj in range(1, 4):
                k.stt("dve", dst, cand[jj], sel[:, jj:jj + 1], dst, ALU.mult, ALU.add, [rcand[jj], rsel, rob[i]], [rob[i]])

    load_chunk(0, 0)
    for lc in range(4):
        i = lc % 2
        if lc + 1 < 4:
            load_chunk(lc + 1, 1 - i)
        if layer == 0:
            for fc in range(KC):
                j = fc % 2
                bk = 2 + j
                k.act(sq[j], ob[i][:, fc, :], AF.Square, [rob[i]], [rsq[j]])
                k.mm(k.bank(bk), k.ones_f, sq[j], True, True, [k.cres, rsq[j]], [k.rb[bk]])
                k.ts("dve", on[:, fc, :], k.bank(bk), 1.0 / 128, EPS, ALU.mult, ALU.add, [k.rb[bk]], [ron[fc]])
            for fc in range(KC):
                k.act(on[:, fc, :], on[:, fc, :], AF.Sqrt, [ron[fc]], [ron[fc]])
            for fc in range(KC):
                k.recip(on[:, fc, :], on[:, fc, :], [ron[fc]], [ron[fc]])
                k.stt("dve", on[:, fc, :], ob[i][:, fc, :], gs, on[:, fc, :], ALU.mult, ALU.mult, [rob[i], rgs, ron[fc]], [ron[fc]])
        for fc in range(KC):
            j = fc % 2
            bk = j
            for kc in range(KC):
                k.mm(k.bank(bk), Wz[:, kc, fc * 128:(fc + 1) * 128], hb[i][:, kc, :], kc == 0, kc == KC - 1,
                     [rW, rhb[i]], [k.rb[bk]], signal=(kc == KC - 1))
            k.act(sz[j], k.bank(bk), AF.Silu, [k.rb[bk]], [rsz[j]])
            if layer == 0:
                k.tt("dve", og[i][:, fc, :], sz[j], on[:, fc, :], ALU.mult, [rsz[j], ron[fc]], [rog[i]])
            else:
                k.tt("dve", og[i][:, fc, :], sz[j], ob[i][:, fc, :], ALU.mult, [rsz[j], rob[i]], [rog[i]])
        for ts_ in range(4):
            tt = lc * 4 + ts_
            j = tt % 2
            j3 = tt % 3
            P.dma("sp", xt[j], x_d[tt * 128:(tt + 1) * 128, :], [], [rxt[j]])
            b0 = 4 + 2 * j
            for nh in range(2):
                for fc in range(KC):
                    k.mm(k.bank(b0 + nh), og[i][:, fc, ts_ * 128:(ts_ + 1) * 128], Wo[:, fc, nh * 512:(nh + 1) * 512],
                         fc == 0, fc == KC - 1, [rW, rog[i]], [k.rb[b0 + nh]], signal=(fc == KC - 1))
            k.tt("dve", tmp, k.ps[:, b0 * 512:(b0 + 2) * 512], gate, ALU.mult, [k.rb[b0], k.rb[b0 + 1], rgate], [rtmp])
            k.tt("dve", xn[j3], tmp, xt[j], ALU.add, [rtmp, rxt[j]], [rxn[j3]])
            tile_cb[0](tt, xn[j3], rxn[j3])
            if tt >= 1:
                tile_cb[1](tt - 1)
            if tt >= 2:
                tile_cb[2](tt - 2, xn[(tt - 2) % 3], rxn[(tt - 2) % 3])
    tile_cb[1](15)
    tile_cb[2](14, xn[14 % 3], rxn[14 % 3])
    tile_cb[2](15, xn[15 % 3], rxn[15 % 3])


def build_L3(k=None):
    k = k or K("L3")
    x = k.din("x", [TO, D])
    oT = k.din("oT", [D, TO], BF16)
    hT0 = k.din("hT0", [D, TO], BF16)
    wz = k.din("wz", [D, D])
    wo = k.din("wo", [D, D])
    modrow0 = k.din("modrow0", [1, 3 * D])
    subln = k.din("subln", [128, 1])
    cvec = k.din("cvec", [128, KC])
    ada_w = k.din("ada_w", [D, 3 * D])
    ada_b = k.din("ada_b", [1, 3 * D])
    norm_g = k.din("norm_g", [1, D])
    w1a = k.din("w1a", [D, 704])
    gqkv = k.din("gqkv", [128, 5])
    pos_own = k.din("pos_own", [1, TO], I32)
    cst_d = k.din("cst", [128, 8])
    ident = k.din("ident", [128, 128])
    x1_o = k.dout("x1", [TO, D])
    hT1_o = k.dout("hT1", [D, TO], BF16)
    GT_o = k.dout("GT", [768, TO], BF16)
    if k.fused:
        hT1_pc = lambda c: k.ext["hT1_c"][c]
        GT_pc = lambda c: k.ext["GT_c"][c]
    else:
        hT1_pc = lambda c: hT1_o[:, c * CH:(c + 1) * CH]
        GT_pc = lambda c: GT_o[:, c * CH:(c + 1) * CH]
    modrow1 = k.dout("modrow1", [1, 3 * D])
    cos_d = k.dint("cosT", [128, TO])
    sin_d = k.dint("sinT", [128, TO])
    A, P = k.A, k.P
    k.load_consts(ident)
    cst = A.alloc([8], F32)
    rcst = Res()
    P.dma("sp", cst, cst_d[:, :], [], [rcst])
    rtab = Res()
    rmod1 = Res()
    if not k.fused:
        phase_tables(k, pos_own, cst, rcst, cos_d, sin_d, TO, rtab)
        phase_mod(k, cvec, ada_w, ada_b, norm_g, modrow1, rmod1)

    pn = PreNorm(k, modrow1, rmod1)
    pn.tpb = (2, 3)
    hTs = [A.alloc([KC, CH], BF16) for _ in range(2)]
    rhT = [Res(), Res()]
    rhT1_dram = [Res() for _ in range(4)]

    def cb0(tt, xn, rxn):
        P.dma("act", x1_o[tt * 128:(tt + 1) * 128, :], xn, [rxn], [])
        pn.s0(tt, xn, rxn)

    def cb2(tt, xn, rxn):
        c = tt // 4
        j = c % 2
        pn.s2(tt, xn, rxn, hTs[j][:, :, (tt % 4) * 128:(tt % 4 + 1) * 128], rhT[j])
        if tt % 4 == 3:
            P.dma("sp", hT1_pc(c).rearrange("(kc p) t -> p kc t", p=128), hTs[j], [rhT[j]], [rhT1_dram[c]])

    tile_cb = (cb0, pn.s1, cb2)

    mC = A.mark()
    phase_C(k, 0, x, oT, hT0, wz, wo, modrow0, Res(), subln, tile_cb)
    P.barrier()
    A.release(mC)

    W1 = A.alloc([KC, 768], BF16)
    rW = Res()
    m = A.mark()
    stg = A.alloc([KC, 704], F32)
    rstg = Res()
    P.dma("sp", stg, w_view(w1a, 0, 704), [], [rstg])
    k.copy("dve", W1[:, :, 0:704], stg, [rstg], [rW])
    sv = stg[:, :, 640:704].rearrange("p kc (two h) -> p kc two h", two=2)
    dv = W1[:, :, 704:768].rearrange("p kc (two h) -> p kc two h", two=2)
    k.ts("dve", dv[:, :, 0, :], sv[:, :, 1, :], -1.0, None, ALU.mult, None, [rstg], [rW])
    k.copy("dve", dv[:, :, 1, :], sv[:, :, 0, :], [rstg], [rW])
    P.barrier()
    A.release(m)
    gq = A.alloc([5], F32)
    rgq = Res()
    P.dma("sp", gq, gqkv[:, :], [], [rgq])
    GT = A.alloc([6, TO], BF16)
    rGT = Res()
    P.op("dve", lambda e: e.memset(GT[:, 5, :], 0.0), [], [rGT])
    hb = [A.alloc([KC, CH], BF16) for _ in range(2)]
    rhb = [Res(), Res()]
    sq = [A.alloc([CH], F32) for _ in range(3)]
    rsq = [Res(), Res(), Res()]
    rbt = A.alloc([CH], F32)
    rrbt = Res()
    cb = A.alloc([CH], F32)
    sb_ = A.alloc([CH], F32)
    rtb = Res()
    rtb2 = Res()
    t1 = A.alloc([CH], F32)
    t2 = A.alloc([CH], F32)
    rt1, rt2 = Res(), Res()
    for lc in range(4):
        i = lc % 2
        P.dma("sp", hb[i], hT1_pc(lc).rearrange("(kc p) t -> p kc t", p=128), [rhT1_dram[lc]], [rhb[i]])
        P.dma("act", cb, cos_d[:, lc * CH:(lc + 1) * CH], [rtab], [rtb])
        P.dma("act", sb_, sin_d[:, lc * CH:(lc + 1) * CH], [rtab], [rtb2])
        for (c0, nj, j0, g0, dim) in ((0, 3, 0, 0, 384), (384, 2, 3, 3, 256)):
            for j in range(nj):
                for kc in range(KC):
                    k.mm(k.bank(j), W1[:, kc, c0 + j * 128:c0 + (j + 1) * 128], hb[i][:, kc, :], kc == 0, kc == KC - 1,
                         [rW, rhb[i]], [k.rb[j]], signal=(kc == KC - 1))
                k.act(sq[j], k.bank(j), AF.Square, [k.rb[j]], [rsq[j]])
            for j in range(nj):
                k.mm(k.bank(3), k.ones_f, sq[j], j == 0, j == nj - 1, [k.cres, rsq[j]], [k.rb[3]], signal=(j == nj - 1))
            k.ts("dve", rbt, k.bank(3), 1.0 / dim, EPS, ALU.mult, ALU.add, [k.rb[3]], [rrbt])
            k.act(rbt, rbt, AF.Sqrt, [rrbt], [rrbt])
            k.recip(rbt, rbt, [rrbt], [rrbt])
            for j in range(nj):
                k.stt("dve", GT[:, j0 + j, lc * CH:(lc + 1) * CH], k.bank(j), gq[:, g0 + j:g0 + j + 1], rbt, ALU.mult, ALU.mult,
                      [k.rb[j], rgq, rrbt], [rGT])
        for (bk, c0) in ((4, 640), (5, 704)):
            for kc in range(KC):
                k.mm(k.bank(bk)[0:64, :], W1[:, kc, c0:c0 + 64], hb[i][:, kc, :], kc == 0, kc == KC - 1,
                     [rW, rhb[i]], [k.rb[bk]], signal=(kc == KC - 1))
        k.tt("dve", t1[0:64], k.bank(4)[0:64, :], cb[0:64], ALU.mult, [k.rb[4], rtb], [rt1])
        k.tt("dve", t2[0:64], k.bank(5)[0:64, :], sb_[0:64], ALU.mult, [k.rb[5], rtb2], [rt2])
        k.tt("dve", GT[0:64, 5, lc * CH:(lc + 1) * CH], t1[0:64], t2[0:64], ALU.add, [rt1, rt2], [rGT])
        rpc = Res()
        P.dma("sp", GT_pc(lc).rearrange("(j p) t -> p j t", p=128), GT[:, :, lc * CH:(lc + 1) * CH], [rGT], [rpc])
        if k.fused:
            k.gather_piece("G", lc, k.ext["GT_c"][lc], k.ext["G_dst"][lc], [rpc])
    return k.done()


def build_L4(k=None):
    k = k or K("L4")
    G = k.din("G", [4 * 768, TO], BF16)
    pos = k.din("pos", [1, S], I32)
    cst_d = k.din("cst", [128, 8])
    wqn_d = k.din("wqn", [384, 256])
    wqp_d = k.din("wqp", [384, 128])
    wkk_d = k.din("wkk", [256, 256])
    wkv_d = k.din("wkv", [256, 256])
    oT_o = k.dout("oT", [256, S], BF16)
    if k.fused:
        G_pc = lambda r, lc: k.ext["G_c"][lc][r * 768:(r + 1) * 768, :]
        oT_pc = lambda h, qb: k.ext["oT_c"][qb // 4][h * 128:(h + 1) * 128, (qb % 4) * CH:(qb % 4 + 1) * CH]
    else:
        G_pc = lambda r, lc: G[r * 768:(r + 1) * 768, lc * CH:(lc + 1) * CH]
        oT_pc = lambda h, qb: oT_o[h * 128:(h + 1) * 128, qb * CH:(qb + 1) * CH]
    cos_d = k.dint("cosT", [128, S])
    sin_d = k.dint("sinT", [128, S])
    A, P = k.A, k.P
    k.load_consts(None, need_ident=False)
    cst = A.alloc([8], F32)
    rcst = Res()
    P.dma("sp", cst, cst_d[:, :], [], [rcst])
    rtab = Res()
    if not k.tables_done:
        phase_tables(k, pos, cst, rcst, cos_d, sin_d, S, rtab)
        k.tables_done = k.fused

    Wqn = A.alloc([3, 256], BF16)
    Wqp = A.alloc([3, 128], BF16)
    Wqpr = A.alloc([3, 128], BF16)
    Wkk = A.alloc([2, 256], BF16)
    Wkv = A.alloc([2, 256], BF16)
    rW = Res()
    m = A.mark()
    s1 = A.alloc([3, 256], F32)
    s2 = A.alloc([3, 128], F32)
    s3 = A.alloc([2, 256], F32)
    s4 = A.alloc([2, 256], F32)
    rs = [Res() for _ in range(4)]
    P.dma("sp", s1, wqn_d.rearrange("(j p) n -> p j n", p=128), [], [rs[0]])
    P.dma("sp", s2, wqp_d.rearrange("(j p) n -> p j n", p=128), [], [rs[1]])
    P.dma("pool", s3, wkk_d.rearrange("(j p) n -> p j n", p=128), [], [rs[2]])
    P.dma("pool", s4, wkv_d.rearrange("(j p) n -> p j n", p=128), [], [rs[3]])
    k.copy("dve", Wqn, s1, [rs[0]], [rW])
    k.copy("dve", Wqp, s2, [rs[1]], [rW])
    sv = s2.rearrange("p j (b two h) -> p (j b) two h", two=2, h=32)
    dv = Wqpr.rearrange("p j (b two h) -> p (j b) two h", two=2, h=32)
    k.ts("dve", dv[:, :, 0, :], sv[:, :, 1, :], -1.0, None, ALU.mult, None, [rs[1]], [rW])
    k.copy("dve", dv[:, :, 1, :], sv[:, :, 0, :], [rs[1]], [rW])
    k.copy("dve", Wkk, s3, [rs[2]], [rW])
    k.copy("dve", Wkv, s4, [rs[3]], [rW])
    P.barrier()
    A.release(m)

    KT = A.alloc([2, S], BF16)
    KPE = A.alloc([S], BF16)
    V = A.alloc([64, 256], BF16)
    rKT, rKPE, rV = Res(), Res(), Res()
    gb = [A.alloc([5, CH], BF16) for _ in range(2)]
    rgb = [Res(), Res()]
    cb = [A.alloc([CH], F32) for _ in range(2)]
    sb_ = [A.alloc([CH], F32) for _ in range(2)]
    rtb = [Res(), Res()]
    rtb2 = [Res(), Res()]

    def load_chunk(ck, i, tabs):
        r, lc = ck // 4, ck % 4
        P.dma("sp", gb[i], G_pc(r, lc)[0:640, :].rearrange("(j p) t -> p j t", p=128), [k.gres["G"][lc]] if k.fused else [], [rgb[i]])
        if tabs:
            P.dma("act", cb[i], cos_d[:, ck * CH:(ck + 1) * CH], [rtab], [rtb[i]])
            P.dma("act", sb_[i], sin_d[:, ck * CH:(ck + 1) * CH], [rtab], [rtb2[i]])

    order = [r * 4 + lc for lc in range(4) for r in range(4)] if os.environ.get("KORD", "1") == "1" else list(range(16))
    load_chunk(order[0], 0, False)
    for n_ in range(16):
        ck = order[n_]
        i = n_ % 2
        if n_ + 1 < 16:
            load_chunk(order[n_ + 1], 1 - i, False)
        for h in range(2):
            bk = h
            for j in range(2):
                k.mm(k.bank(bk), Wkk[:, j, h * 128:(h + 1) * 128], gb[i][:, 3 + j, :], j == 0, j == 1,
                     [rW, rgb[i]], [k.rb[bk]], signal=(j == 1))
            k.copy("dve", KT[:, h, ck * CH:(ck + 1) * CH], k.bank(bk), [k.rb[bk]], [rKT])
        for ts_ in range(4):
            bk = 4 + ts_ % 2
            for j in range(2):
                k.mm(k.bank(bk, 256), gb[i][:, 3 + j, ts_ * 128:(ts_ + 1) * 128], Wkv[:, j, :], j == 0, j == 1,
                     [rW, rgb[i]], [k.rb[bk]], signal=(j == 1))
            k.copy("act", V[:, ck * 4 + ts_, :], k.bank(bk, 256), [k.rb[bk]], [rV])

    rKPE_all = []
    for lc in range(4):
        gr = [k.gres["G"][lc]] if k.fused else []
        for hh, q in ((0, "sp"), (1, "act")):
            dst = KPE[hh * 64:(hh + 1) * 64, :].rearrange("p (r l t) -> p r l t", r=4, l=4)[:, :, lc, :]
            if k.fused:
                srcv = k.ext["G_c"][lc].rearrange("(r q) t -> q r t", q=768)[640:704]
            else:
                srcv = G[:, lc * CH:(lc + 1) * CH].rearrange("(r q) t -> q r t", q=768)[640:704]
            rk_ = Res()
            P.dma(q, dst, srcv, gr, [rk_])
            rKPE_all.append(rk_)

    QN = [A.alloc([2, CH], BF16) for _ in range(2)]
    QP = [A.alloc([CH], BF16) for _ in range(2)]
    rQ = [Res(), Res()]
    PT = [A.alloc([2 * CH], BF16) for _ in range(4)]
    rPT = [Res() for _ in range(4)]
    acc = [A.alloc([2 * CH], F32) for _ in range(2)]
    racc = [Res(), Res()]
    t1 = A.alloc([CH], F32)
    t2 = A.alloc([CH], F32)
    rt1, rt2 = Res(), Res()
    r1 = [A.alloc([CH], F32) for _ in range(2)]
    rr1 = [Res(), Res()]
    ob = [A.alloc([CH], BF16) for _ in range(2)]
    rob = [Res(), Res()]
    scale = 192 ** -0.5
    load_chunk(0, 0, True)
    cnt = 0
    piece_res = []
    wpre = WPrefetch(k, k.ext["wz_next"], k.ext["wo_next"]) if (k.fused and os.environ.get("KPRE", "1") == "1") else None
    for qb in range(16):
        i = qb % 2
        if qb + 1 < 16:
            load_chunk(qb + 1, 1 - i, True)
        if wpre is not None and qb >= 8:
            wpre.step()
        for h in range(2):
            for j in range(3):
                k.mm(k.bank(h), Wqn[:, j, h * 128:(h + 1) * 128], gb[i][:, j, :], j == 0, j == 2,
                     [rW, rgb[i]], [k.rb[h]], signal=(j == 2))
            k.copy("dve", QN[i][:, h, :], k.bank(h), [k.rb[h]], [rQ[i]])
        for (bk, W) in ((2, Wqp), (3, Wqpr)):
            for j in range(3):
                k.mm(k.bank(bk), W[:, j, :], gb[i][:, j, :], j == 0, j == 2, [rW, rgb[i]], [k.rb[bk]], signal=(j == 2))
        k.tt("dve", t1, k.bank(2), cb[i], ALU.mult, [k.rb[2], rtb[i]], [rt1])
        k.tt("dve", t2, k.bank(3), sb_[i], ALU.mult, [k.rb[3], rtb2[i]], [rt2])
        k.tt("dve", QP[i], t1, t2, ALU.add, [rt1, rt2], [rQ[i]])
        qn, qp, rq = QN[i], QP[i], rQ[i]

        def qk_fn(u, b0, qn=qn, qp=qp, rq=rq):
            ksl = slice(u * 128, (u + 1) * 128)
            k.mm(k.bank(b0), KPE[0:64, ksl], qp[0:64, :], True, False, rKPE_all + [rq], [k.rb[b0]], signal=False)
            k.mm(k.bank(b0 + 1), KPE[64:128, ksl], qp[64:128, :], True, False, rKPE_all + [rq], [k.rb[b0 + 1]], signal=False)
            k.mm(k.bank(b0), KT[:, 0, ksl], qn[:, 0, :], False, True, [rKT, rq], [k.rb[b0]], signal=False)
            k.mm(k.bank(b0 + 1), KT[:, 1, ksl], qn[:, 1, :], False, True, [rKT, rq], [k.rb[b0 + 1]], signal=True)

        aset = qb % 2

        def pv_fn(u, pt, rpt, aset=aset):
            st, sp = (u == 0), (u == 63)
            k.mm(k.bank(4), V[:, u, 0:128], pt[:, 0:CH], st, sp, [rV, rpt], [k.rb[4]], signal=False)
            k.mm(k.bank(5), V[:, u, 128:256], pt[:, CH:2 * CH], st, sp, [rV, rpt], [k.rb[5]], signal=True)
            if st:
                k.copy("dve", acc[aset], pt, [rpt], [racc[aset]])
            else:
                k.tt("dve", acc[aset], acc[aset], pt, ALU.add, [rpt, racc[aset]], [racc[aset]])

        attention_block(k, 64, qk_fn, pv_fn, scale, PT, rPT, warm0=int(os.environ.get("KWARM", "20")), warm_bank=0)
        for h in range(2):
            k.mm(k.bank(6 + h), k.ones_f, acc[aset][:, h * CH:(h + 1) * CH], True, True, [k.cres, racc[aset]], [k.rb[6 + h]])
        for h in range(2):
            k.recip(r1[h], k.bank(6 + h), [k.rb[6 + h]], [rr1[h]])
            j = cnt % 2
            cnt += 1
            k.tt("dve", ob[j], k.bank(4 + h), r1[h], ALU.mult, [k.rb[4 + h], rr1[h]], [rob[j]])
            rpc = Res()
            P.dma("sp", oT_pc(h, qb), ob[j], [rob[j]], [rpc])
            piece_res.append(rpc)
        if k.fused and qb % 4 == 3:
            k.gather_piece("OT", qb // 4, k.ext["oT_c"][qb // 4], k.ext["OT_dst"][qb // 4], piece_res)
            piece_res = []
    return k.done()


def build_L5(k=None):
    k = k or K("L5")
    x = k.din("x", [TO, D])
    oT = k.din("oT", [D, TO], BF16)
    hT1 = k.din("hT1", [D, TO], BF16)
    wz = k.din("wz", [D, D])
    wo = k.din("wo", [D, D])
    modrow1 = k.din("modrow1", [1, 3 * D])
    fin_g = k.din("fin_g", [1, D])
    out = k.dout("out", [TO, D])
    A, P = k.A, k.P
    k.load_consts(None, need_ident=False)
    gf = A.alloc([D], F32)
    rgf = Res()
    P.dma("sp", gf, fin_g[0:1, :].partition_broadcast(128), [], [rgf])
    junk = A.alloc([D], F32)
    rjunk = Res()
    ss = A.alloc([16 * 16], F32)
    rss = [Res() for _ in range(16)]
    P.op("pool", lambda e: e.memset(ss, 0.0), [], rss)
    yo = [A.alloc([D], F32) for _ in range(2)]
    ryo = [Res(), Res()]

    def cb0(tt, xn, rxn):
        k.act(junk, xn, AF.Square, [rxn], [rss[tt]], accum_out=ss[:, tt * 16:tt * 16 + 1])

    def cb1(tt):
        k.rsqrt_inplace(ss[:, tt * 16:tt * 16 + 1], rss[tt], 1.0 / D, EPS)

    def cb2(tt, xn, rxn):
        j = tt % 2
        k.stt("dve", yo[j], xn, ss[:, tt * 16:tt * 16 + 1], gf, ALU.mult, ALU.mult, [rxn, rss[tt], rgf], [ryo[j]])
        P.dma("act", out[tt * 128:(tt + 1) * 128, :], yo[j], [ryo[j]], [])

    tile_cb = (cb0, cb1, cb2)

    phase_C(k, 1, x, oT, hT1, wz, wo, modrow1, Res(), None, tile_cb)
    return k.done()


def all_gather(k, src, dst):
    P = k.P
    P.barrier()
    P.op("pool", lambda e: e.collective_compute("AllGather", ALU.bypass, replica_groups=GROUPS, ins=[src.opt()], outs=[dst.opt()]), [], [])
    P.barrier()


def build_fused():
    k = K("F")
    k.fused = True
    nc = k.nc
    I = lambda n, s, dt=F32: nc.dram_tensor(n, list(s), dt, kind="ExternalInput").ap()
    T = lambda n, s, dt=F32: nc.dram_tensor(n, list(s), dt).ap()
    x = I("x", [TO, D]); cvec = I("cvec", [128, KC]); pos = I("pos", [1, S], I32); pos_own = I("pos_own", [1, TO], I32)
    cst = I("cst", [128, 8]); ident = I("ident", [128, 128]); sel = I("sel", [128, 4])
    ada_w0 = I("ada_w0", [D, 3 * D]); ada_b0 = I("ada_b0", [1, 3 * D]); norm_g0 = I("norm_g0", [1, D])
    wq0 = I("wq0", [D, 256]); wk0 = I("wk0", [D, 256]); wv0 = I("wv0", [D, 256]); lamv = I("lamv", [1, 256])
    wz0 = I("wz0", [D, D]); wo0 = I("wo0", [D, D]); subln = I("subln", [128, 1])
    ada_w1 = I("ada_w1", [D, 3 * D]); ada_b1 = I("ada_b1", [1, 3 * D]); norm_g1 = I("norm_g1", [1, D])
    w1a = I("w1a", [D, 704]); gqkv = I("gqkv", [128, 5])
    wqn = I("wqn", [384, 256]); wqp = I("wqp", [384, 128]); wkk = I("wkk", [256, 256]); wkv = I("wkv", [256, 256])
    wz1 = I("wz1", [D, D]); wo1 = I("wo1", [D, D]); fin_g = I("fin_g", [1, D])
    out = nc.dram_tensor("out", [TO, D], F32, kind="ExternalOutput").ap()
    hT0_c = [T("hT0_c%d" % i, [D, CH], BF16) for i in range(4)]
    HT0_c = [T("HT0_c%d" % i, [4 * D, CH], BF16) for i in range(4)]
    oT_c = [T("oT_c%d" % i, [256, TO], BF16) for i in range(4)]
    OT_c = [T("OT_c%d" % i, [D, TO], BF16) for i in range(4)]
    hT1_c = [T("hT1_c%d" % i, [D, CH], BF16) for i in range(4)]
    GT_c = [T("GT_c%d" % i, [768, CH], BF16) for i in range(4)]
    G_c = [T("G_c%d" % i, [4 * 768, CH], BF16) for i in range(4)]
    modrow0 = T("modrow0", [1, 3 * D]); modrow1 = T("modrow1", [1, 3 * D]); x1 = T("x1", [TO, D])
    cosT = T("cosT_all", [128, S]); sinT = T("sinT_all", [128, S]); cosO = T("cosT_own", [128, TO]); sinO = T("sinT_own", [128, TO])
    dummy = T("dummy_x", [1, 64])
    k.load_consts(ident)
    k.Wz_pre = k.A.alloc([KC, D], BF16)
    k.Wo_pre = k.A.alloc([KC, D], BF16)
    k.rWpre = Res()
    k.base_mark = k.A.mark()

    def setE(**kw):
        k.ext.clear()
        k.ext["sel"] = sel
        k.ext.update(kw)

    for nm in ("HT", "OT", "G"):
        k.gres[nm] = [Res() for _ in range(4)]

    stop = int(os.environ.get("KSTOP", "99"))
    setE(x=x, cvec=cvec, ada_w=ada_w0, ada_b=ada_b0, norm_g=norm_g0, ident=ident, hT=dummy, hT_c=hT0_c, HT_dst=HT0_c, modrow=modrow0,
         early=dict(pos=pos, pos_own=pos_own, cst=cst, cosT=cosT, sinT=sinT, cosO=cosO, sinO=sinO,
                    ada_w1=ada_w1, ada_b1=ada_b1, norm_g1=norm_g1, modrow1=modrow1))
    build_L1(k)
    if stop == 0:
        return k.finish()
    if stop == 1:
        return k.finish()
    setE(HT=dummy, HT_c=HT0_c, pos=pos, cst=cst, wq=wq0, wk=wk0, wv=wv0, lamv=lamv, ident=ident, oT=dummy, oT_c=oT_c, OT_dst=OT_c, cosT=cosT, sinT=sinT, wz_next=wz0, wo_next=wo0,
         early=dict(pos_own=pos_own, cosO=cosO, sinO=sinO, cvec=cvec, ada_w1=ada_w1, ada_b1=ada_b1, norm_g1=norm_g1, modrow1=modrow1))
    build_L2(k)
    if stop == 2:
        return k.finish()
    if stop == 3:
        return k.finish()
    setE(x=x, oT=OT_c, hT0=hT0_c, wz=wz0, wo=wo0, modrow0=modrow0, subln=subln, cvec=cvec, ada_w=ada_w1, ada_b=ada_b1,
         norm_g=norm_g1, w1a=w1a, gqkv=gqkv, pos_own=pos_own, cst=cst, ident=ident, x1=x1, hT1=dummy, hT1_c=hT1_c, GT=dummy, GT_c=GT_c, G_dst=G_c,
         modrow1=modrow1, cosT=cosO, sinT=sinO)
    build_L3(k)
    if stop == 4:
        return k.finish()
    setE(G=dummy, G_c=G_c, pos=pos, cst=cst, wqn=wqn, wqp=wqp, wkk=wkk, wkv=wkv, oT=dummy, oT_c=oT_c, OT_dst=OT_c, cosT=cosT, sinT=sinT, wz_next=wz1, wo_next=wo1)
    build_L4(k)
    setE(x=x1, oT=OT_c, hT1=hT1_c, wz=wz1, wo=wo1, modrow1=modrow1, fin_g=fin_g, out=out)
    build_L5(k)
    return k.finish()


_NC = {}


def _get(name, fn):
    if name not in _NC:
        _NC[name] = fn()
    return _NC[name]


def _run(nc, in_maps):
    res = run_bass_kernel_spmd(nc, in_maps, core_ids=list(range(8)))
    return res.results


def kernel(x, c, positions,
           ada_w0, ada_b0, norm_g0, w_in0, lam_q1, lam_k1, lam_q2, lam_k2, subln_g, w_out0,
           ada_w1, ada_b1, norm_g1, w_in1, q_a_norm_g, w_q_b, kv_a_norm_g, w_kv_b, w_out1,
           final_norm_g):
    f = lambda a: np.ascontiguousarray(np.asarray(a), dtype=np.float32)
    x = f(x); c = f(c)
    positions = np.ascontiguousarray(np.asarray(positions), dtype=np.int32)
    ada_w0, ada_b0, norm_g0, w_in0, w_out0 = map(f, (ada_w0, ada_b0, norm_g0, w_in0, w_out0))
    ada_w1, ada_b1, norm_g1, w_in1, w_out1 = map(f, (ada_w1, ada_b1, norm_g1, w_in1, w_out1))
    w_q_b, w_kv_b = f(w_q_b), f(w_kv_b)
    ident = np.eye(128, dtype=np.float32)
    cst = np.zeros((128, 8), np.float32)
    cst[:, 0] = (10000.0 ** (-(np.arange(128) % 32).astype(np.float32) / 32.0)).astype(np.float32)
    lamv = np.concatenate([f(lam_q1), f(lam_k1), f(lam_q2), f(lam_k2)])[None, :]
    gqkv = np.ascontiguousarray(np.concatenate([f(q_a_norm_g).reshape(3, 128).T, f(kv_a_norm_g).reshape(2, 128).T], 1))
    wqb = w_q_b.reshape(384, 8, 192)
    wkvb = w_kv_b.reshape(256, 8, 256)
    ca = np.ascontiguousarray
    shared = dict(cst=cst, ident=ident, ada_w0=ada_w0, ada_b0=ada_b0[None, :], norm_g0=norm_g0[None, :], lamv=lamv,
                  wz0=ca(w_in0[:, 3072:4096]), wo0=w_out0, subln=f(subln_g)[:, None],
                  ada_w1=ada_w1, ada_b1=ada_b1[None, :], norm_g1=norm_g1[None, :], w1a=ca(w_in1[:, 0:704]), gqkv=gqkv,
                  wz1=ca(w_in1[:, 704:1728]), wo1=w_out1, fin_g=f(final_norm_g)[None, :])
    in_maps = []
    for cc in range(8):
        b, j = cc // 4, cc % 4
        sl = slice(j * 256, (j + 1) * 256)
        hs = slice(2 * j, 2 * j + 2)
        selv = np.zeros((128, 4), np.float32)
        selv[:, j] = 1.0
        m = dict(shared)
        m.update(x=ca(x[b, j * TO:(j + 1) * TO]), cvec=ca(c[b].reshape(KC, 128).T), pos=ca(positions[b][None, :]),
                 pos_own=ca(positions[b][None, j * TO:(j + 1) * TO]), sel=selv,
                 wq0=ca(w_in0[:, 0:1024][:, sl]), wk0=ca(w_in0[:, 1024:2048][:, sl]), wv0=ca(w_in0[:, 2048:3072][:, sl]),
                 wqn=ca(wqb[:, hs, 0:128].reshape(384, 256)), wqp=ca(wqb[:, hs, 128:192].reshape(384, 128)),
                 wkk=ca(wkvb[:, hs, 0:128].reshape(256, 256)), wkv=ca(wkvb[:, hs, 128:256].reshape(256, 256)))
        in_maps.append(m)
    res = _run(_get("F", build_fused), in_maps)
    out = np.empty((2, S, D), np.float32)
    for cc in range(8):
        b, j = cc // 4, cc % 4
        out[b, j * TO:(j + 1) * TO] = res[cc]["out"]
    return out
```

```python
import os
from contextlib import ExitStack

import numpy as np
import ml_dtypes

import concourse.bass as bass
import concourse.mybir as mybir
from concourse.bass_utils import run_bass_kernel_spmd

GROUPS = [[0, 1, 2, 3], [4, 5, 6, 7]]
F32 = mybir.dt.float32
BF16 = mybir.dt.bfloat16
I32 = mybir.dt.int32
AF = mybir.ActivationFunctionType
ALU = mybir.AluOpType
AX = mybir.AxisListType

D = 1024
S = 8192
TO = 2048
CH = 512
KC = 8
EPS = 1e-6
LAMBDA_INIT0 = 0.8 - 0.6 * 1.0
TWO_PI = 6.283185307179586
CW1 = 6.28125
CW2 = TWO_PI - CW1
HALF_PI = 1.5707963267948966


class Res:
    __slots__ = ("name", "w", "r")

    def __init__(self, name=""):
        self.name = name
        self.w = None
        self.r = {}


class Prog:
    ENGS = ("pe", "act", "dve", "pool", "sp")

    def __init__(self, nc, st, n_dma_sems=12):
        self.nc = nc
        self.q = {e: [] for e in self.ENGS}
        self.esem = {e: st.enter_context(nc.semaphore("s_" + e)) for e in self.ENGS}
        self.tick = {e: 0 for e in self.ENGS}
        self.pending = {e: False for e in self.ENGS}
        self.known = {e: {} for e in self.ENGS}
        self.dsem = {}
        for e in ("sp", "act", "pool"):
            self.dsem[e] = [[st.enter_context(nc.semaphore("d_%s%d" % (e, i))), 0] for i in range(n_dma_sems)]
        self.drr = {e: 0 for e in self.dsem}
        self.ccsem = st.enter_context(nc.semaphore("cc_done"))
        self.ccval = 0

    def _waits(self, e, reads, writes):
        my = self.esem[e]
        need = {}

        def add(sv, same_ok):
            if sv is None:
                return
            s, v = sv
            if same_ok and s is my:
                return
            k = id(s)
            if k not in need or need[k][1] < v:
                need[k] = (s, v)

        for r in reads:
            add(r.w, False)
        for w in writes:
            add(w.w, True)
            for sv in w.r.values():
                add(sv, True)
        out = []
        kn = self.known[e]
        for k, (s, v) in need.items():
            if kn.get(k, 0) >= v:
                continue
            kn[k] = v
            out.append((s, v))
        return out

    @staticmethod
    def _mark(reads, writes, s, v):
        for r in reads:
            old = r.r.get(id(s))
            if old is None or old[1] < v:
                r.r[id(s)] = (s, v)
        for w in writes:
            w.w = (s, v)
            w.r = {}

    def op(self, e, fn, reads=(), writes=(), signal=True):
        waits = self._waits(e, reads, writes)
        s = self.esem[e]
        if signal:
            self.tick[e] += 1
            v = self.tick[e]
            self.pending[e] = False
            inc = (s, 1)
        else:
            v = self.tick[e] + 1
            self.pending[e] = True
            inc = None
        self._mark(reads, writes, s, v)
        self.q[e].append((waits, fn, inc))

    def dma(self, e, out, in_, reads=(), writes=(), **kw):
        lst = self.dsem[e]
        i = self.drr[e]
        self.drr[e] = (i + 1) % len(lst)
        ent = lst[i]
        s = ent[0]
        waits = self._waits(e, reads, writes)
        prev = ent[1]
        kn = self.known[e]
        if prev > 0 and kn.get(id(s), 0) < prev:
            kn[id(s)] = prev
            waits.append((s, prev))
        ent[1] = prev + 16
        v = ent[1]
        self._mark(reads, writes, s, v)
        self.q[e].append((waits, lambda eng: eng.dma_start(out=out, in_=in_, **kw), (s, 16)))

    def cc(self, fn, reads=(), writes=()):
        waits = self._waits("pool", reads, writes)
        inc = 1
        self.ccval += inc
        self._mark(reads, writes, self.ccsem, self.ccval)
        self.q["pool"].append((waits, fn, (self.ccsem, inc)))

    def barrier(self, skip=()):
        pts = []
        for e in self.ENGS:
            assert not self.pending[e]
            if self.tick[e] > 0 and e not in skip:
                pts.append((self.esem[e], self.tick[e]))
        for e in self.dsem:
            for s, v in self.dsem[e]:
                if v > 0:
                    pts.append((s, v))
        if self.ccval > 0 and "pool" not in skip:
            pts.append((self.ccsem, self.ccval))
        for e in self.ENGS:
            kn = self.known[e]
            waits = []
            for s, v in pts:
                if s is self.esem[e]:
                    continue
                if kn.get(id(s), 0) < v:
                    kn[id(s)] = v
                    waits.append((s, v))
            if waits:
                self.q[e].append((waits, None, None))

    def emit(self):
        nc = self.nc
        for e in self.ENGS:
            assert not self.pending[e], e
        with nc.Block() as block:
            def run(eng, lst):
                for waits, fn, inc in lst:
                    for s, v in waits:
                        eng.wait_ge(s, v)
                    if fn is not None:
                        ins = fn(eng)
                        if inc is not None:
                            ins.then_inc(inc[0], inc[1])

            @block.tensor
            def _(eng):
                run(eng, self.q["pe"])

            @block.scalar
            def _(eng):
                run(eng, self.q["act"])

            @block.vector
            def _(eng):
                run(eng, self.q["dve"])

            @block.gpsimd
            def _(eng):
                run(eng, self.q["pool"])

            @block.sync
            def _(eng):
                run(eng, self.q["sp"])


class Arena:
    def __init__(self, nc, st, nbytes):
        self.t = st.enter_context(nc.sbuf_tensor("arena", [128, nbytes // 2], BF16))
        self.n = nbytes
        self.off = 0

    def alloc(self, free_shape, dt, parts=128):
        esz = 4 if dt in (F32, I32) else 2
        n = 1
        for d in free_shape:
            n *= d
        nb = (n * esz + 63) // 64 * 64
        assert self.off + nb <= self.n, ("arena overflow", self.off, nb, self.n)
        ap = self.t[:, self.off // 2:(self.off + n * esz) // 2]
        self.off += nb
        self.peak = max(getattr(self, "peak", 0), self.off)
        if dt != BF16:
            ap = ap.bitcast(dt)
        if len(free_shape) == 2:
            ap = ap.rearrange("p (a b) -> p a b", b=free_shape[1])
        elif len(free_shape) == 3:
            ap = ap.rearrange("p (a b c) -> p a b c", b=free_shape[1], c=free_shape[2])
        if parts != 128:
            ap = ap[0:parts]
        return ap

    def mark(self):
        return self.off

    def release(self, m):
        self.off = m


class K:
    def __init__(self, name):
        self.ext = {}
        self.gres = {}
        self.fused = False
        self.consts_loaded = False
        self.tables_done = False
        self.nc = bass.Bass("TRN2", target_bir_lowering=False)
        self.st = ExitStack()
        self.P = Prog(self.nc, self.st)
        self.A = Arena(self.nc, self.st, 200 * 1024)
        self.ps = self.st.enter_context(self.nc.psum_tensor("ps", [128, 4096], F32))
        self.rb = [Res("bank%d" % i) for i in range(8)]
        self.cres = Res("consts")

    def bank(self, i, n=512):
        return self.ps[:, i * 512:i * 512 + n]

    def bank_bf(self, i):
        return self.ps[:, i * 512:(i + 1) * 512].bitcast(BF16)

    def din(self, name, shape, dt=F32):
        if name in self.ext:
            return self.ext[name]
        return self.nc.dram_tensor(name, list(shape), dt, kind="ExternalInput").ap()

    def dout(self, name, shape, dt=F32):
        if name in self.ext:
            return self.ext[name]
        return self.nc.dram_tensor(name, list(shape), dt, kind="ExternalOutput").ap()

    def dint(self, name, shape, dt=F32):
        if name in self.ext:
            return self.ext[name]
        return self.nc.dram_tensor(name, list(shape), dt).ap()

    def done(self):
        if not self.fused:
            return self.finish()
        self.P.barrier(skip=("pool",))
        if os.environ.get("KDBG"):
            print("arena peak", self.A.peak, "tick", dict(self.P.tick))
        self.A.peak = 0
        self.A.release(self.base_mark)
        return None

    def gather_piece(self, name, idx, src, dst, reads):
        self.P.cc(lambda e: e.collective_compute("AllGather", ALU.bypass, replica_groups=GROUPS, ins=[src.opt()], outs=[dst.opt()]),
                  reads, [self.gres[name][idx]])

    def finish(self):
        self.P.barrier()
        self.P.emit()
        self.st.close()
        return self.nc

    def mm(self, out, lhsT, rhs, start, stop, reads, writes, signal=True):
        self.P.op("pe", lambda e: e.matmul(out, lhsT=lhsT, rhs=rhs, start=start, stop=stop), reads, writes, signal)

    def act(self, out, in_, func, reads, writes, **kw):
        self.P.op("act", lambda e: e.activation(out=out, in_=in_, func=func, **kw), reads, writes)

    def copy(self, eng, out, in_, reads, writes):
        if eng == "act":
            self.P.op(eng, lambda e: e.activation(out=out, in_=in_, func=AF.Copy), reads, writes)
        else:
            self.P.op(eng, lambda e: e.tensor_copy(out=out, in_=in_), reads, writes)

    def tt(self, eng, out, in0, in1, op, reads, writes):
        self.P.op(eng, lambda e: e.tensor_tensor(out=out, in0=in0, in1=in1, op=op), reads, writes)

    def ts(self, eng, out, in0, s1, s2, op0, op1, reads, writes):
        if s2 is None:
            self.P.op(eng, lambda e: e.tensor_scalar(out=out, in0=in0, scalar1=s1, scalar2=None, op0=op0), reads, writes)
        else:
            self.P.op(eng, lambda e: e.tensor_scalar(out=out, in0=in0, scalar1=s1, scalar2=s2, op0=op0, op1=op1), reads, writes)

    def stt(self, eng, out, in0, scalar, in1, op0, op1, reads, writes):
        self.P.op(eng, lambda e: e.scalar_tensor_tensor(out=out, in0=in0, scalar=scalar, in1=in1, op0=op0, op1=op1), reads, writes)

    def warm(self, n, bank=7):
        for i in range(n):
            self.mm(self.bank(bank), self.ones_b, self.warm_rhs, True, True, [self.cres], [self.rb[bank]], signal=(i == n - 1))

    def recip(self, out, in_, reads, writes):
        self.P.op("dve", lambda e: e.reciprocal(out=out, in_=in_), reads, writes)

    def rsqrt_inplace(self, ap, res, mul, add):
        self.ts("dve", ap, ap, mul, add, ALU.mult, ALU.add, [res], [res])
        self.act(ap, ap, AF.Sqrt, [res], [res])
        self.recip(ap, ap, [res], [res])

    def load_consts(self, ident_d, need_ident=True):
        A, P = self.A, self.P
        if self.consts_loaded:
            return
        self.consts_loaded = True
        self.ones_b = A.alloc([128], BF16)
        self.ones_f = A.alloc([128], F32)
        P.op("pool", lambda e: e.memset(self.ones_b, 1.0), [], [self.cres])
        P.op("pool", lambda e: e.memset(self.ones_f, 1.0), [], [self.cres])
        self.warm_rhs = A.alloc([512], BF16)
        P.op("pool", lambda e: e.memset(self.warm_rhs, 1.0), [], [self.cres])
        if need_ident:
            idf = A.alloc([128], F32)
            self.ident_b = A.alloc([128], BF16)
            r = Res()
            P.dma("sp", idf, ident_d[:, :], [], [r])
            self.copy("dve", self.ident_b, idf, [r], [self.cres])


def w_view(w, c0, c1):
    return w[:, c0:c1].rearrange("(kc p) n -> p kc n", p=128)


def phase_mod(k, cvec_d, ada_w, ada_b, norm_g, modrow_d, mod_res, sync=True):
    A, P = k.A, k.P
    m = A.mark()
    cv = A.alloc([KC], F32)
    sc = A.alloc([KC], F32)
    wb = [A.alloc([KC, 512], F32) for _ in range(2)]
    row = A.alloc([3072], F32, parts=1)
    brow = A.alloc([3072], F32, parts=1)
    grow = A.alloc([1024], F32, parts=1)
    rcv, rsc, rrow, rbrow, rgrow = Res(), Res(), Res(), Res(), Res()
    rwb = [Res(), Res()]
    P.dma("sp", cv, cvec_d[:, :], [], [rcv])
    P.dma("sp", brow, ada_b[:, :], [], [rbrow])
    P.dma("sp", grow, norm_g[:, :], [], [rgrow])
    k.act(sc, cv, AF.Silu, [rcv], [rsc])
    for n in range(6):
        i = n % 2
        P.dma("sp" if n % 2 == 0 else "pool", wb[i], w_view(ada_w, n * 512, (n + 1) * 512), [], [rwb[i]])
        bk = n % 2
        for kc in range(KC):
            k.mm(k.bank(bk)[0:1, :], sc[:, kc:kc + 1], wb[i][:, kc, :], kc == 0, kc == KC - 1,
                 [rsc, rwb[i]], [k.rb[bk]], signal=(kc == KC - 1))
        k.tt("dve", row[:, n * 512:(n + 1) * 512], k.bank(bk)[0:1, :], brow[:, n * 512:(n + 1) * 512], ALU.add,
             [k.rb[bk], rbrow], [rrow])
    k.stt("dve", row[:, 1024:2048], row[:, 1024:2048], 1.0, grow, ALU.add, ALU.mult, [rrow, rgrow], [rrow])
    P.dma("sp", modrow_d[:, :], row, [rrow], [mod_res])
    if sync:
        k.P.barrier()
        A.release(m)


def phase_tables(k, pos_d, cst, rcst, cos_d, sin_d, ntok, tab_res, sync=True):
    for _ in tables_gen(k, pos_d, cst, rcst, cos_d, sin_d, ntok, tab_res, sync):
        pass


def tables_gen(k, pos_d, cst, rcst, cos_d, sin_d, ntok, tab_res, sync=True, bufs=None):
    A, P = k.A, k.P
    m = A.mark()
    C = 2048
    posi = A.alloc([C], I32)
    a = A.alloc([C], F32)
    kf = A.alloc([C], F32)
    ki = A.alloc([C], I32)
    so = A.alloc([C], F32)
    co = A.alloc([C], F32)
    rp, ra, rk, rki, rso, rco = Res(), Res(), Res(), Res(), Res(), Res()
    for c in range(ntok // C):
        P.dma("sp", posi, pos_d[0:1, c * C:(c + 1) * C].partition_broadcast(128), [], [rp])
        k.copy("dve", a, posi, [rp], [ra])
        k.ts("dve", a, a, cst[:, 0:1], None, ALU.mult, None, [ra, rcst], [ra])
        k.ts("dve", kf, a, 1.0 / TWO_PI, None, ALU.mult, None, [ra], [rk])
        k.copy("dve", ki, kf, [rk], [rki])
        k.copy("dve", kf, ki, [rki], [rk])
        k.stt("dve", a, kf, -CW1, a, ALU.mult, ALU.add, [rk, ra], [ra])
        k.stt("dve", a, kf, -CW2, a, ALU.mult, ALU.add, [rk, ra], [ra])
        k.ts("dve", a, a, 3.1415925, -3.1415925, ALU.min, ALU.max, [ra], [ra])
        k.act(so, a, AF.Sin, [ra], [rso])
        k.ts("dve", kf, a, HALF_PI, -TWO_PI, ALU.is_gt, ALU.mult, [ra], [rk])
        k.stt("dve", kf, a, HALF_PI, kf, ALU.add, ALU.add, [ra, rk], [rk])
        k.act(co, kf, AF.Sin, [rk], [rco])
        P.dma("sp", sin_d[:, c * C:(c + 1) * C], so, [rso], [tab_res])
        P.dma("act", cos_d[:, c * C:(c + 1) * C], co, [rco], [tab_res])
        yield c
    if sync:
        k.P.barrier()
        A.release(m)


class PreNorm:
    def __init__(self, k, modrow_d, mod_res):
        A, P = k.A, k.P
        self.k = k
        self.gmod = A.alloc([D], F32)
        self.shift = A.alloc([D], F32)
        self.rmod = Res()
        self.rshift = Res()
        P.dma("sp", self.shift, modrow_d[0:1, 0:D].partition_broadcast(128), [mod_res], [self.rshift])
        P.dma("pool", self.gmod, modrow_d[0:1, D:2 * D].partition_broadcast(128), [mod_res], [self.rmod])
        self.junk = A.alloc([D], F32)
        self.rjunk = Res()
        self.NB = 3
        self.tmp = [A.alloc([D], F32) for _ in range(self.NB)]
        self.rtmp = [Res() for _ in range(self.NB)]
        self.ss = A.alloc([16 * 16], F32)
        self.rss = [Res() for _ in range(16)]
        P.op("dve", lambda e: e.memset(self.ss, 0.0), [], self.rss)
        self.hb = [A.alloc([D], BF16) for _ in range(self.NB)]
        self.rhb = [Res() for _ in range(self.NB)]
        self.tpb = (6, 7)

    def s0(self, tt, xt, rx):
        self.k.act(self.junk, xt, AF.Square, [rx, self.rjunk], [self.rss[tt], self.rjunk], accum_out=self.ss[:, tt * 16:tt * 16 + 1])

    def s1(self, tt):
        self.k.rsqrt_inplace(self.ss[:, tt * 16:tt * 16 + 1], self.rss[tt], 1.0 / D, EPS)

    def s2(self, tt, xt, rx, hT_dst, r_hT):
        k = self.k
        i = tt % self.NB
        ssc = self.ss[:, tt * 16:tt * 16 + 1]
        k.stt("dve", self.tmp[i], xt, ssc, self.gmod, ALU.mult, ALU.mult, [rx, self.rss[tt], self.rmod], [self.rtmp[i]])
        k.tt("dve", self.hb[i], self.tmp[i], self.shift, ALU.add, [self.rtmp[i], self.rshift], [self.rhb[i]])
        bk = self.tpb[tt % 2]
        tp = k.bank_bf(bk)
        for kc in range(KC):
            k.P.op("pe", (lambda kc=kc: (lambda e: e.transpose(tp[:, kc * 128:(kc + 1) * 128], self.hb[i][:, kc * 128:(kc + 1) * 128], k.ident_b)))(),
                   [self.rhb[i], k.cres], [k.rb[bk]], signal=(kc == KC - 1))
        k.copy("act", hT_dst, tp.rearrange("p (kc t) -> p kc t", t=128), [k.rb[bk]], [r_hT])


def run_pipeline(n, stages):
    ns = len(stages)
    for step in range(n + ns - 1):
        for s in range(ns):
            t = step - s
            if 0 <= t < n:
                stages[s](t)


def build_L1(k=None):
    k = k or K("L1")
    x = k.din("x", [TO, D])
    cvec = k.din("cvec", [128, KC])
    ada_w = k.din("ada_w", [D, 3 * D])
    ada_b = k.din("ada_b", [1, 3 * D])
    norm_g = k.din("norm_g", [1, D])
    ident = k.din("ident", [128, 128])
    hT_o = k.dout("hT", [D, TO], BF16)
    hT_pc = (lambda c: k.ext["hT_c"][c]) if k.fused else (lambda c: hT_o[:, c * CH:(c + 1) * CH])
    modrow = k.dout("modrow", [1, 3 * D])
    k.load_consts(ident)
    rmod = Res()
    phase_mod(k, cvec, ada_w, ada_b, norm_g, modrow, rmod)
    A, P = k.A, k.P
    pn = PreNorm(k, modrow, rmod)
    xb = [A.alloc([D], F32) for _ in range(4)]
    rxb = [Res() for _ in range(4)]
    hTs = [A.alloc([KC, CH], BF16) for _ in range(2)]
    rhT = [Res(), Res()]
    gens = []
    mtab = A.mark()
    if k.fused:
        X = k.ext["early"]
        cst = A.alloc([8], F32)
        rcst = Res()
        P.dma("sp", cst, X["cst"][:, :], [], [rcst])
        gens = [tables_gen(k, X["pos"], cst, rcst, X["cosT"], X["sinT"], S, Res(), sync=False),
                tables_gen(k, X["pos_own"], cst, rcst, X["cosO"], X["sinO"], TO, Res(), sync=False)]
        k.tables_done = True
    def l1_s0(tt):
        i = tt % 4
        P.dma("sp" if i % 2 == 0 else "act", xb[i], x[tt * 128:(tt + 1) * 128, :], [], [rxb[i]])
        if tt % 3 == 1 and gens:
            if next(gens[0], None) is None:
                gens.pop(0)
        pn.s0(tt, xb[i], rxb[i])

    def l1_s2(tt):
        i = tt % 4
        c = tt // 4
        j = c % 2
        pn.s2(tt, xb[i], rxb[i], hTs[j][:, :, (tt % 4) * 128:(tt % 4 + 1) * 128], rhT[j])
        if tt % 4 == 3:
            rpc = Res()
            P.dma("sp", hT_pc(c).rearrange("(kc p) t -> p kc t", p=128), hTs[j], [rhT[j]], [rpc])
            if k.fused:
                k.gather_piece("HT", c, k.ext["hT_c"][c], k.ext["HT_dst"][c], [rpc])

    run_pipeline(16, [l1_s0, pn.s1, l1_s2])
    for g in gens:
        for _ in g:
            pass
    if k.fused:
        P.barrier(skip=("pool",))
        A.release(mtab)
        X = k.ext["early"]
        phase_mod(k, cvec, X["ada_w1"], X["ada_b1"], X["norm_g1"], X["modrow1"], Res(), sync=False)
    return k.done()


def attention_block(k, n_units, qk_fn, pv_fn, scale, PT, rPT, sbank0=0, warm0=0, warm_bank=7):
    def exp_unit(u):
        b = sbank0 + 2 * (u % 2)
        i = u % len(PT)
        k.act(PT[i], k.ps[:, b * 512:(b + 2) * 512], AF.Exp, [k.rb[b], k.rb[b + 1]], [rPT[i]], scale=scale)

    qk_fn(0, sbank0)
    if n_units > 1:
        qk_fn(1, sbank0 + 2)
    for u in range(n_units):
        exp_unit(u)
        if u == 0 and warm0:
            k.warm(warm0, warm_bank)
        if u + 2 < n_units:
            qk_fn(u + 2, sbank0 + 2 * (u % 2))
        pv_fn(u, PT[u % len(PT)], rPT[u % len(PT)])


def build_L2(k=None):
    k = k or K("L2")
    HT = k.din("HT", [4 * D, TO], BF16)
    pos = k.din("pos", [1, S], I32)
    cst_d = k.din("cst", [128, 8])
    wq_d = k.din("wq", [D, 256])
    wk_d = k.din("wk", [D, 256])
    wv_d = k.din("wv", [D, 256])
    lamv_d = k.din("lamv", [1, 256])
    ident = k.din("ident", [128, 128])
    oT_o = k.dout("oT", [256, S], BF16)
    if k.fused:
        HT_pc = lambda r, lc: k.ext["HT_c"][lc][r * D:(r + 1) * D, :]
        oT_pc = lambda h, qb: k.ext["oT_c"][qb // 4][h * 128:(h + 1) * 128, (qb % 4) * CH:(qb % 4 + 1) * CH]
    else:
        HT_pc = lambda r, lc: HT[r * D:(r + 1) * D, lc * CH:(lc + 1) * CH]
        oT_pc = lambda h, qb: oT_o[h * 128:(h + 1) * 128, qb * CH:(qb + 1) * CH]
    cos_d = k.dint("cosT", [128, S])
    sin_d = k.dint("sinT", [128, S])
    A, P = k.A, k.P
    k.load_consts(ident, need_ident=False)
    cst = A.alloc([8], F32)
    rcst = Res()
    P.dma("sp", cst, cst_d[:, :], [], [rcst])
    rtab = Res()
    if k.fused:
        pass
    elif not k.tables_done:
        phase_tables(k, pos, cst, rcst, cos_d, sin_d, S, rtab)

    lamv = A.alloc([256], F32, parts=1)
    lt = A.alloc([128], F32, parts=1)
    l2 = A.alloc([2], F32, parts=1)
    nl = A.alloc([1], F32, parts=1)
    neglam = A.alloc([1], F32)
    rl, rlt, rl2, rnl, rneg = Res(), Res(), Res(), Res(), Res()
    P.dma("sp", lamv, lamv_d[:, :], [], [rl])
    k.tt("dve", lt[:, 0:64], lamv[:, 0:64], lamv[:, 64:128], ALU.mult, [rl], [rlt])
    k.tt("dve", lt[:, 64:128], lamv[:, 128:192], lamv[:, 192:256], ALU.mult, [rl], [rlt])
    P.op("dve", lambda e: e.reduce_sum(out=l2[:, 0:1], in_=lt[:, 0:64], axis=AX.X), [rlt], [rl2])
    P.op("dve", lambda e: e.reduce_sum(out=l2[:, 1:2], in_=lt[:, 64:128], axis=AX.X), [rlt], [rl2])
    k.act(l2, l2, AF.Exp, [rl2], [rl2])
    k.tt("dve", nl, l2[:, 1:2], l2[:, 0:1], ALU.subtract, [rl2], [rnl])
    k.ts("dve", nl, nl, -LAMBDA_INIT0, None, ALU.add, None, [rnl], [rnl])
    k.mm(k.bank(0)[:, 0:1], k.ones_f[0:1, :], nl[0:1, 0:1], True, True, [k.cres, rnl], [k.rb[0]])
    k.copy("dve", neglam, k.bank(0)[:, 0:1], [k.rb[0]], [rneg])

    Wq = A.alloc([KC, 256], BF16)
    Wqr = A.alloc([KC, 256], BF16)
    Wk = A.alloc([KC, 256], BF16)
    Wkr = A.alloc([KC, 256], BF16)
    Wv = A.alloc([KC, 256], BF16)
    rW = Res()
    m = A.mark()
    stg = [A.alloc([KC, 256], F32) for _ in range(3)]
    rstg = [Res(), Res(), Res()]
    for i, (wd, Wn, Wr) in enumerate(((wq_d, Wq, Wqr), (wk_d, Wk, Wkr), (wv_d, Wv, None))):
        P.dma("sp", stg[i], w_view(wd, 0, 256), [], [rstg[i]])
        k.copy("dve", Wn, stg[i], [rstg[i]], [rW])
        if Wr is not None:
            sv = stg[i].rearrange("p kc (b two h) -> p (kc b) two h", two=2, h=32)
            dv = Wr.rearrange("p kc (b two h) -> p (kc b) two h", two=2, h=32)
            k.ts("dve", dv[:, :, 0, :], sv[:, :, 1, :], -1.0, None, ALU.mult, None, [rstg[i]], [rW])
            k.copy("dve", dv[:, :, 1, :], sv[:, :, 0, :], [rstg[i]], [rW])
    P.barrier()
    A.release(m)

    KT = A.alloc([2, S], BF16)
    V = A.alloc([64, 256], BF16)
    rKT, rV = Res(), Res()
    hb = [A.alloc([KC, CH], BF16) for _ in range(2)]
    rhb = [Res(), Res()]
    cb = [A.alloc([CH], F32) for _ in range(2)]
    sb_ = [A.alloc([CH], F32) for _ in range(2)]
    rtb = [Res(), Res()]
    rtb2 = [Res(), Res()]
    t1 = [A.alloc([CH], F32) for _ in range(2)]
    t2 = [A.alloc([CH], F32) for _ in range(2)]
    rt1 = [Res(), Res()]
    rt2 = [Res(), Res()]

    def load_chunk(ck, i):
        r, lc = ck // 4, ck % 4
        P.dma("sp", hb[i], HT_pc(r, lc).rearrange("(kc p) t -> p kc t", p=128), [k.gres["HT"][lc]] if k.fused else [], [rhb[i]])
        P.dma("act", cb[i], cos_d[:, ck * CH:(ck + 1) * CH], [rtab], [rtb[i]])
        P.dma("act", sb_[i], sin_d[:, ck * CH:(ck + 1) * CH], [rtab], [rtb2[i]])

    def proj_rope(i, W, Wr, h, b0, dst, rdst, j):
        for kc in range(KC):
            k.mm(k.bank(b0), W[:, kc, h * 128:(h + 1) * 128], hb[i][:, kc, :], kc == 0, kc == KC - 1,
                 [rW, rhb[i]], [k.rb[b0]], signal=(kc == KC - 1))
        for kc in range(KC):
            k.mm(k.bank(b0 + 1), Wr[:, kc, h * 128:(h + 1) * 128], hb[i][:, kc, :], kc == 0, kc == KC - 1,
                 [rW, rhb[i]], [k.rb[b0 + 1]], signal=(kc == KC - 1))
        k.tt("dve", t1[j], k.bank(b0), cb[i], ALU.mult, [k.rb[b0], rtb[i]], [rt1[j]])
        k.tt("dve", t2[j], k.bank(b0 + 1), sb_[i], ALU.mult, [k.rb[b0 + 1], rtb2[i]], [rt2[j]])
        k.tt("dve", dst, t1[j], t2[j], ALU.add, [rt1[j], rt2[j]], [rdst])

    order = [r * 4 + lc for lc in range(4) for r in range(4)] if os.environ.get("KORD", "1") == "1" else list(range(16))
    load_chunk(order[0], 0)
    for n_ in range(16):
        ck = order[n_]
        i = n_ % 2
        if n_ + 1 < 16:
            load_chunk(order[n_ + 1], 1 - i)
        for h in range(2):
            proj_rope(i, Wk, Wkr, h, 2 * h, KT[:, h, ck * CH:(ck + 1) * CH], rKT, h)
        for ts_ in range(4):
            bk = 4 + ts_ % 2
            for kc in range(KC):
                k.mm(k.bank(bk, 256), hb[i][:, kc, ts_ * 128:(ts_ + 1) * 128], Wv[:, kc, :], kc == 0, kc == KC - 1,
                     [rW, rhb[i]], [k.rb[bk]], signal=(kc == KC - 1))
            k.copy("act", V[:, ck * 4 + ts_, :], k.bank(bk, 256), [k.rb[bk]], [rV])

    QT = [A.alloc([2, CH], BF16) for _ in range(2)]
    rQT = [[Res(), Res()], [Res(), Res()]]
    PT = [A.alloc([2 * CH], BF16) for _ in range(4)]
    rPT = [Res() for _ in range(4)]
    acc = [A.alloc([2 * CH], F32) for _ in range(2)]
    racc = [Res(), Res()]
    r1 = A.alloc([CH], F32)
    r2 = A.alloc([CH], F32)
    o1 = A.alloc([CH], F32)
    o2 = A.alloc([CH], F32)
    rr1, rr2, ro1, ro2 = Res(), Res(), Res(), Res()
    ob = [A.alloc([CH], BF16) for _ in range(2)]
    rob = [Res(), Res()]
    scale = 64 ** -0.5
    load_chunk(0, 0)
    cnt = 0
    piece_res = []
    wpre = WPrefetch(k, k.ext["wz_next"], k.ext["wo_next"]) if (k.fused and os.environ.get("KPRE", "1") == "1") else None
    for qb in range(16):
        i = qb % 2
        if qb + 1 < 16:
            load_chunk(qb + 1, 1 - i)
        if wpre is not None and qb >= 8:
            wpre.step()
        for h in range(2):
            proj_rope(i, Wq, Wqr, h, 2 * h, QT[i][:, h, :], rQT[i][h], h)
        for h in range(2):
            q_ap = QT[i]
            rq = rQT[i][h]

            def qk_fn(u, b0, h=h, q_ap=q_ap, rq=rq):
                k.mm(k.bank(b0), KT[0:64, h, u * 128:(u + 1) * 128], q_ap[0:64, h, :], True, True, [rKT, rq], [k.rb[b0]])
                k.mm(k.bank(b0 + 1), KT[64:128, h, u * 128:(u + 1) * 128], q_ap[64:128, h, :], True, True, [rKT, rq], [k.rb[b0 + 1]])

            aset = cnt % 2

            def pv_fn(u, pt, rpt, h=h, aset=aset):
                st, sp = (u == 0), (u == 63)
                k.mm(k.bank(4), V[:, u, h * 128:(h + 1) * 128], pt[:, 0:CH], st, sp, [rV, rpt], [k.rb[4]], signal=False)
                k.mm(k.bank(6), V[:, u, h * 128:(h + 1) * 128], pt[:, CH:2 * CH], st, sp, [rV, rpt], [k.rb[6]], signal=True)
                if st:
                    k.copy("dve", acc[aset], pt, [rpt], [racc[aset]])
                elif u % 2 == 1:
                    k.mm(k.bank(5), k.ones_b, pt[:, 0:CH], u == 1, False, [k.cres, rpt], [k.rb[5]])
                    k.tt("dve", acc[aset][:, CH:2 * CH], acc[aset][:, CH:2 * CH], pt[:, CH:2 * CH], ALU.add, [rpt, racc[aset]], [racc[aset]])
                else:
                    k.tt("dve", acc[aset], acc[aset], pt, ALU.add, [rpt, racc[aset]], [racc[aset]])

            attention_block(k, 64, qk_fn, pv_fn, scale, PT, rPT, warm0=int(os.environ.get("KWARM2", "14")), warm_bank=0)
            k.mm(k.bank(5), k.ones_f, acc[aset][:, 0:CH], False, True, [k.cres, racc[aset]], [k.rb[5]])
            k.mm(k.bank(7), k.ones_f, acc[aset][:, CH:2 * CH], True, True, [k.cres, racc[aset]], [k.rb[7]])
            k.recip(r1, k.bank(5), [k.rb[5]], [rr1])
            k.recip(r2, k.bank(7), [k.rb[7]], [rr2])
            k.tt("dve", o1, k.bank(4), r1, ALU.mult, [k.rb[4], rr1], [ro1])
            k.tt("dve", o2, k.bank(6), r2, ALU.mult, [k.rb[6], rr2], [ro2])
            j = cnt % 2
            cnt += 1
            k.stt("dve", ob[j], o2, neglam, o1, ALU.mult, ALU.add, [ro2, ro1, rneg], [rob[j]])
            rpc = Res()
            P.dma("sp", oT_pc(h, qb), ob[j], [rob[j]], [rpc])
            piece_res.append(rpc)
        if k.fused and qb % 4 == 3:
            k.gather_piece("OT", qb // 4, k.ext["oT_c"][qb // 4], k.ext["OT_dst"][qb // 4], piece_res)
            piece_res = []
    return k.done()


class WPrefetch:
    def __init__(self, k, wz_d, wo_d):
        self.k = k
        self.stg = [k.A.alloc([KC, 256], F32) for _ in range(2)]
        self.rstg = [Res(), Res()]
        self.todo = [(wd, Wn, c) for wd, Wn in ((wz_d, k.Wz_pre), (wo_d, k.Wo_pre)) for c in range(4)]
        self.n = 0

    def step(self):
        if not self.todo:
            return
        wd, Wn, c = self.todo.pop(0)
        i = self.n % 2
        self.n += 1
        k = self.k
        k.P.dma("sp" if i == 0 else "act", self.stg[i], w_view(wd, c * 256, (c + 1) * 256), [], [self.rstg[i]])
        k.copy("act", Wn[:, :, c * 256:(c + 1) * 256], self.stg[i], [self.rstg[i]], [k.rWpre])


def phase_C(k, layer, x_d, oT_d, hT_d, wz_d, wo_d, modrow_d, rmod, subln_d, tile_cb):
    A, P = k.A, k.P
    if k.fused and os.environ.get("KPRE", "1") == "1":
        Wz, Wo, rW = k.Wz_pre, k.Wo_pre, k.rWpre
    else:
        Wz = A.alloc([KC, D], BF16)
        Wo = A.alloc([KC, D], BF16)
        rW = Res()
        m = A.mark()
        stg = [A.alloc([KC, 256], F32) for _ in range(2)]
        rstg = [Res(), Res()]
        n = 0
        for wd, Wn in ((wz_d, Wz), (wo_d, Wo)):
            for c in range(4):
                i = n % 2
                n += 1
                P.dma("sp" if i == 0 else "pool", stg[i], w_view(wd, c * 256, (c + 1) * 256), [], [rstg[i]])
                k.copy("dve" if i == 0 else "act", Wn[:, :, c * 256:(c + 1) * 256], stg[i], [rstg[i]], [rW])
        P.barrier()
        A.release(m)
    gate = A.alloc([D], F32)
    rgate = Res()
    P.dma("sp", gate, modrow_d[0:1, 2 * D:3 * D].partition_broadcast(128), [rmod], [rgate])
    if layer == 0:
        gs = A.alloc([1], F32)
        rgs = Res()
        P.dma("sp", gs, subln_d[:, :], [], [rgs])
        k.ts("dve", gs, gs, 1.0 - LAMBDA_INIT0, None, ALU.mult, None, [rgs], [rgs])
        on = A.alloc([KC, CH], F32)
        ron = [Res() for _ in range(KC)]
        sq = [A.alloc([CH], F32) for _ in range(2)]
        rsq = [Res(), Res()]
    hb = [A.alloc([KC, CH], BF16) for _ in range(2)]
    rhb = [Res(), Res()]
    ob = [A.alloc([KC, CH], BF16) for _ in range(2)]
    rob = [Res(), Res()]
    sz = [A.alloc([CH], F32) for _ in range(2)]
    rsz = [Res(), Res()]
    og = [A.alloc([KC, CH], BF16) for _ in range(2)]
    rog = [Res(), Res()]
    xt = [A.alloc([D], F32) for _ in range(2)]
    rxt = [Res(), Res()]
    tmp = A.alloc([D], F32)
    rtmp = Res()
    xn = [A.alloc([D], F32) for _ in range(3)]
    rxn = [Res() for _ in range(3)]
    tail = []

    if k.fused:
        sel = A.alloc([4], F32)
        rsel = Res()
        P.dma("sp", sel, k.ext["sel"][:, :], [], [rsel])
        cand = [A.alloc([4, CH], BF16) for _ in range(4)]
        rcand = [Res() for _ in range(4)]

    def load_chunk(lc, i):
        if k.fused:
            P.dma("sp", hb[i], hT_d[lc].rearrange("(kc p) t -> p kc t", p=128), [], [rhb[i]])
        else:
            P.dma("sp", hb[i], hT_d[:, lc * CH:(lc + 1) * CH].rearrange("(kc p) t -> p kc t", p=128), [], [rhb[i]])
        if not k.fused:
            P.dma("pool", ob[i], oT_d[:, lc * CH:(lc + 1) * CH].rearrange("(kc p) t -> p kc t", p=128), [], [rob[i]])
            return
        for half in range(2):
            dst = ob[i][:, half * 4:(half + 1) * 4, :]
            for jj in range(4):
                P.dma("pool" if jj % 2 == 0 else "sp", cand[jj],
                      oT_d[jj][:, lc * CH:(lc + 1) * CH].rearrange("(kc p) t -> p kc t", p=128)[:, half * 4:(half + 1) * 4, :], [k.gres["OT"][jj]], [rcand[jj]])
            k.ts("dve", dst, cand[0], sel[:, 0:1], None, ALU.mult, None, [rcand[0], rsel], [rob[i]])
            for jj in range(1, 4):
                k.stt("dve", dst, cand[jj], sel[:, jj:jj + 1], dst, ALU.mult, ALU.add, [rcand[jj], rsel, rob[i]], [rob[i]])

    load_chunk(0, 0)
    for lc in range(4):
        i = lc % 2
        if lc + 1 < 4:
            load_chunk(lc + 1, 1 - i)
        if layer == 0:
            for fc in range(KC):
                j = fc % 2
                bk = 2 + j
                k.act(sq[j], ob[i][:, fc, :], AF.Square, [rob[i]], [rsq[j]])
                k.mm(k.bank(bk), k.ones_f, sq[j], True, True, [k.cres, rsq[j]], [k.rb[bk]])
                k.ts("dve", on[:, fc, :], k.bank(bk), 1.0 / 128, EPS, ALU.mult, ALU.add, [k.rb[bk]], [ron[fc]])
            for fc in range(KC):
                k.act(on[:, fc, :], on[:, fc, :], AF.Sqrt, [ron[fc]], [ron[fc]])
            for fc in range(KC):
                k.recip(on[:, fc, :], on[:, fc, :], [ron[fc]], [ron[fc]])
                k.stt("dve", on[:, fc, :], ob[i][:, fc, :], gs, on[:, fc, :], ALU.mult, ALU.mult, [rob[i], rgs, ron[fc]], [ron[fc]])
        for fc in range(KC):
            j = fc % 2
            bk = j
            for kc in range(KC):
                k.mm(k.bank(bk), Wz[:, kc, fc * 128:(fc + 1) * 128], hb[i][:, kc, :], kc == 0, kc == KC - 1,
                     [rW, rhb[i]], [k.rb[bk]], signal=(kc == KC - 1))
            k.act(sz[j], k.bank(bk), AF.Silu, [k.rb[bk]], [rsz[j]])
            if layer == 0:
                k.tt("dve", og[i][:, fc, :], sz[j], on[:, fc, :], ALU.mult, [rsz[j], ron[fc]], [rog[i]])
            else:
                k.tt("dve", og[i][:, fc, :], sz[j], ob[i][:, fc, :], ALU.mult, [rsz[j], rob[i]], [rog[i]])
        for ts_ in range(4):
            tt = lc * 4 + ts_
            j = tt % 2
            j3 = tt % 3
            P.dma("sp", xt[j], x_d[tt * 128:(tt + 1) * 128, :], [], [rxt[j]])
            b0 = 4 + 2 * j
            for nh in range(2):
                for fc in range(KC):
                    k.mm(k.bank(b0 + nh), og[i][:, fc, ts_ * 128:(ts_ + 1) * 128], Wo[:, fc, nh * 512:(nh + 1) * 512],
                         fc == 0, fc == KC - 1, [rW, rog[i]], [k.rb[b0 + nh]], signal=(fc == KC - 1))
            k.tt("dve", tmp, k.ps[:, b0 * 512:(b0 + 2) * 512], gate, ALU.mult, [k.rb[b0], k.rb[b0 + 1], rgate], [rtmp])
            k.tt("dve", xn[j3], tmp, xt[j], ALU.add, [rtmp, rxt[j]], [rxn[j3]])
            tile_cb[0](tt, xn[j3], rxn[j3])
            if tt >= 1:
                tile_cb[1](tt - 1)
            if tt >= 2:
                tile_cb[2](tt - 2, xn[(tt - 2) % 3], rxn[(tt - 2) % 3])
    tile_cb[1](15)
    tile_cb[2](14, xn[14 % 3], rxn[14 % 3])
    tile_cb[2](15, xn[15 % 3], rxn[15 % 3])


def build_L3(k=None):
    k = k or K("L3")
    x = k.din("x", [TO, D])
    oT = k.din("oT", [D, TO], BF16)
    hT0 = k.din("hT0", [D, TO], BF16)
    wz = k.din("wz", [D, D])
    wo = k.din("wo", [D, D])
    modrow0 = k.din("modrow0", [1, 3 * D])
    subln = k.din("subln", [128, 1])
    cvec = k.din("cvec", [128, KC])
    ada_w = k.din("ada_w", [D, 3 * D])
    ada_b = k.din("ada_b", [1, 3 * D])
    norm_g = k.din("norm_g", [1, D])
    w1a = k.din("w1a", [D, 704])
    gqkv = k.din("gqkv", [128, 5])
    pos_own = k.din("pos_own", [1, TO], I32)
    cst_d = k.din("cst", [128, 8])
    ident = k.din("ident", [128, 128])
    x1_o = k.dout("x1", [TO, D])
    hT1_o = k.dout("hT1", [D, TO], BF16)
    GT_o = k.dout("GT", [768, TO], BF16)
    if k.fused:
        hT1_pc = lambda c: k.ext["hT1_c"][c]
        GT_pc = lambda c: k.ext["GT_c"][c]
    else:
        hT1_pc = lambda c: hT1_o[:, c * CH:(c + 1) * CH]
        GT_pc = lambda c: GT_o[:, c * CH:(c + 1) * CH]
    modrow1 = k.dout("modrow1", [1, 3 * D])
    cos_d = k.dint("cosT", [128, TO])
    sin_d = k.dint("sinT", [128, TO])
    A, P = k.A, k.P
    k.load_consts(ident)
    cst = A.alloc([8], F32)
    rcst = Res()
    P.dma("sp", cst, cst_d[:, :], [], [rcst])
    rtab = Res()
    rmod1 = Res()
    if not k.fused:
        phase_tables(k, pos_own, cst, rcst, cos_d, sin_d, TO, rtab)
        phase_mod(k, cvec, ada_w, ada_b, norm_g, modrow1, rmod1)

    pn = PreNorm(k, modrow1, rmod1)
    pn.tpb = (2, 3)
    hTs = [A.alloc([KC, CH], BF16) for _ in range(2)]
    rhT = [Res(), Res()]
    rhT1_dram = [Res() for _ in range(4)]

    def cb0(tt, xn, rxn):
        P.dma("act", x1_o[tt * 128:(tt + 1) * 128, :], xn, [rxn], [])
        pn.s0(tt, xn, rxn)

    def cb2(tt, xn, rxn):
        c = tt // 4
        j = c % 2
        pn.s2(tt, xn, rxn, hTs[j][:, :, (tt % 4) * 128:(tt % 4 + 1) * 128], rhT[j])
        if tt % 4 == 3:
            P.dma("sp", hT1_pc(c).rearrange("(kc p) t -> p kc t", p=128), hTs[j], [rhT[j]], [rhT1_dram[c]])

    tile_cb = (cb0, pn.s1, cb2)

    mC = A.mark()
    phase_C(k, 0, x, oT, hT0, wz, wo, modrow0, Res(), subln, tile_cb)
    P.barrier()
    A.release(mC)

    W1 = A.alloc([KC, 768], BF16)
    rW = Res()
    m = A.mark()
    stg = A.alloc([KC, 704], F32)
    rstg = Res()
    P.dma("sp", stg, w_view(w1a, 0, 704), [], [rstg])
    k.copy("dve", W1[:, :, 0:704], stg, [rstg], [rW])
    sv = stg[:, :, 640:704].rearrange("p kc (two h) -> p kc two h", two=2)
    dv = W1[:, :, 704:768].rearrange("p kc (two h) -> p kc two h", two=2)
    k.ts("dve", dv[:, :, 0, :], sv[:, :, 1, :], -1.0, None, ALU.mult, None, [rstg], [rW])
    k.copy("dve", dv[:, :, 1, :], sv[:, :, 0, :], [rstg], [rW])
    P.barrier()
    A.release(m)
    gq = A.alloc([5], F32)
    rgq = Res()
    P.dma("sp", gq, gqkv[:, :], [], [rgq])
    GT = A.alloc([6, TO], BF16)
    rGT = Res()
    P.op("dve", lambda e: e.memset(GT[:, 5, :], 0.0), [], [rGT])
    hb = [A.alloc([KC, CH], BF16) for _ in range(2)]
    rhb = [Res(), Res()]
    sq = [A.alloc([CH], F32) for _ in range(3)]
    rsq = [Res(), Res(), Res()]
    rbt = A.alloc([CH], F32)
    rrbt = Res()
    cb = A.alloc([CH], F32)
    sb_ = A.alloc([CH], F32)
    rtb = Res()
    rtb2 = Res()
    t1 = A.alloc([CH], F32)
    t2 = A.alloc([CH], F32)
    rt1, rt2 = Res(), Res()
    for lc in range(4):
        i = lc % 2
        P.dma("sp", hb[i], hT1_pc(lc).rearrange("(kc p) t -> p kc t", p=128), [rhT1_dram[lc]], [rhb[i]])
        P.dma("act", cb, cos_d[:, lc * CH:(lc + 1) * CH], [rtab], [rtb])
        P.dma("act", sb_, sin_d[:, lc * CH:(lc + 1) * CH], [rtab], [rtb2])
        for (c0, nj, j0, g0, dim) in ((0, 3, 0, 0, 384), (384, 2, 3, 3, 256)):
            for j in range(nj):
                for kc in range(KC):
                    k.mm(k.bank(j), W1[:, kc, c0 + j * 128:c0 + (j + 1) * 128], hb[i][:, kc, :], kc == 0, kc == KC - 1,
                         [rW, rhb[i]], [k.rb[j]], signal=(kc == KC - 1))
                k.act(sq[j], k.bank(j), AF.Square, [k.rb[j]], [rsq[j]])
            for j in range(nj):
                k.mm(k.bank(3), k.ones_f, sq[j], j == 0, j == nj - 1, [k.cres, rsq[j]], [k.rb[3]], signal=(j == nj - 1))
            k.ts("dve", rbt, k.bank(3), 1.0 / dim, EPS, ALU.mult, ALU.add, [k.rb[3]], [rrbt])
            k.act(rbt, rbt, AF.Sqrt, [rrbt], [rrbt])
            k.recip(rbt, rbt, [rrbt], [rrbt])
            for j in range(nj):
                k.stt("dve", GT[:, j0 + j, lc * CH:(lc + 1) * CH], k.bank(j), gq[:, g0 + j:g0 + j + 1], rbt, ALU.mult, ALU.mult,
                      [k.rb[j], rgq, rrbt], [rGT])
        for (bk, c0) in ((4, 640), (5, 704)):
            for kc in range(KC):
                k.mm(k.bank(bk)[0:64, :], W1[:, kc, c0:c0 + 64], hb[i][:, kc, :], kc == 0, kc == KC - 1,
                     [rW, rhb[i]], [k.rb[bk]], signal=(kc == KC - 1))
        k.tt("dve", t1[0:64], k.bank(4)[0:64, :], cb[0:64], ALU.mult, [k.rb[4], rtb], [rt1])
        k.tt("dve", t2[0:64], k.bank(5)[0:64, :], sb_[0:64], ALU.mult, [k.rb[5], rtb2], [rt2])
        k.tt("dve", GT[0:64, 5, lc * CH:(lc + 1) * CH], t1[0:64], t2[0:64], ALU.add, [rt1, rt2], [rGT])
        rpc = Res()
        P.dma("sp", GT_pc(lc).rearrange("(j p) t -> p j t", p=128), GT[:, :, lc * CH:(lc + 1) * CH], [rGT], [rpc])
        if k.fused:
            k.gather_piece("G", lc, k.ext["GT_c"][lc], k.ext["G_dst"][lc], [rpc])
    return k.done()


def build_L4(k=None):
    k = k or K("L4")
    G = k.din("G", [4 * 768, TO], BF16)
    pos = k.din("pos", [1, S], I32)
    cst_d = k.din("cst", [128, 8])
    wqn_d = k.din("wqn", [384, 256])
    wqp_d = k.din("wqp", [384, 128])
    wkk_d = k.din("wkk", [256, 256])
    wkv_d = k.din("wkv", [256, 256])
    oT_o = k.dout("oT", [256, S], BF16)
    if k.fused:
        G_pc = lambda r, lc: k.ext["G_c"][lc][r * 768:(r + 1) * 768, :]
        oT_pc = lambda h, qb: k.ext["oT_c"][qb // 4][h * 128:(h + 1) * 128, (qb % 4) * CH:(qb % 4 + 1) * CH]
    else:
        G_pc = lambda r, lc: G[r * 768:(r + 1) * 768, lc * CH:(lc + 1) * CH]
        oT_pc = lambda h, qb: oT_o[h * 128:(h + 1) * 128, qb * CH:(qb + 1) * CH]
    cos_d = k.dint("cosT", [128, S])
    sin_d = k.dint("sinT", [128, S])
    A, P = k.A, k.P
    k.load_consts(None, need_ident=False)
    cst = A.alloc([8], F32)
    rcst = Res()
    P.dma("sp", cst, cst_d[:, :], [], [rcst])
    rtab = Res()
    if not k.tables_done:
        phase_tables(k, pos, cst, rcst, cos_d, sin_d, S, rtab)
        k.tables_done = k.fused

    Wqn = A.alloc([3, 256], BF16)
    Wqp = A.alloc([3, 128], BF16)
    Wqpr = A.alloc([3, 128], BF16)
    Wkk = A.alloc([2, 256], BF16)
    Wkv = A.alloc([2, 256], BF16)
    rW = Res()
    m = A.mark()
    s1 = A.alloc([3, 256], F32)
    s2 = A.alloc([3, 128], F32)
    s3 = A.alloc([2, 256], F32)
    s4 = A.alloc([2, 256], F32)
    rs = [Res() for _ in range(4)]
    P.dma("sp", s1, wqn_d.rearrange("(j p) n -> p j n", p=128), [], [rs[0]])
    P.dma("sp", s2, wqp_d.rearrange("(j p) n -> p j n", p=128), [], [rs[1]])
    P.dma("pool", s3, wkk_d.rearrange("(j p) n -> p j n", p=128), [], [rs[2]])
    P.dma("pool", s4, wkv_d.rearrange("(j p) n -> p j n", p=128), [], [rs[3]])
    k.copy("dve", Wqn, s1, [rs[0]], [rW])
    k.copy("dve", Wqp, s2, [rs[1]], [rW])
    sv = s2.rearrange("p j (b two h) -> p (j b) two h", two=2, h=32)
    dv = Wqpr.rearrange("p j (b two h) -> p (j b) two h", two=2, h=32)
    k.ts("dve", dv[:, :, 0, :], sv[:, :, 1, :], -1.0, None, ALU.mult, None, [rs[1]], [rW])
    k.copy("dve", dv[:, :, 1, :], sv[:, :, 0, :], [rs[1]], [rW])
    k.copy("dve", Wkk, s3, [rs[2]], [rW])
    k.copy("dve", Wkv, s4, [rs[3]], [rW])
    P.barrier()
    A.release(m)

    KT = A.alloc([2, S], BF16)
    KPE = A.alloc([S], BF16)
    V = A.alloc([64, 256], BF16)
    rKT, rKPE, rV = Res(), Res(), Res()
    gb = [A.alloc([5, CH], BF16) for _ in range(2)]
    rgb = [Res(), Res()]
    cb = [A.alloc([CH], F32) for _ in range(2)]
    sb_ = [A.alloc([CH], F32) for _ in range(2)]
    rtb = [Res(), Res()]
    rtb2 = [Res(), Res()]

    def load_chunk(ck, i, tabs):
        r, lc = ck // 4, ck % 4
        P.dma("sp", gb[i], G_pc(r, lc)[0:640, :].rearrange("(j p) t -> p j t", p=128), [k.gres["G"][lc]] if k.fused else [], [rgb[i]])
        if tabs:
            P.dma("act", cb[i], cos_d[:, ck * CH:(ck + 1) * CH], [rtab], [rtb[i]])
            P.dma("act", sb_[i], sin_d[:, ck * CH:(ck + 1) * CH], [rtab], [rtb2[i]])

    order = [r * 4 + lc for lc in range(4) for r in range(4)] if os.environ.get("KORD", "1") == "1" else list(range(16))
    load_chunk(order[0], 0, False)
    for n_ in range(16):
        ck = order[n_]
        i = n_ % 2
        if n_ + 1 < 16:
            load_chunk(order[n_ + 1], 1 - i, False)
        for h in range(2):
            bk = h
            for j in range(2):
                k.mm(k.bank(bk), Wkk[:, j, h * 128:(h + 1) * 128], gb[i][:, 3 + j, :], j == 0, j == 1,
                     [rW, rgb[i]], [k.rb[bk]], signal=(j == 1))
            k.copy("dve", KT[:, h, ck * CH:(ck + 1) * CH], k.bank(bk), [k.rb[bk]], [rKT])
        for ts_ in range(4):
            bk = 4 + ts_ % 2
            for j in range(2):
                k.mm(k.bank(bk, 256), gb[i][:, 3 + j, ts_ * 128:(ts_ + 1) * 128], Wkv[:, j, :], j == 0, j == 1,
                     [rW, rgb[i]], [k.rb[bk]], signal=(j == 1))
            k.copy("act", V[:, ck * 4 + ts_, :], k.bank(bk, 256), [k.rb[bk]], [rV])

    rKPE_all = []
    for lc in range(4):
        gr = [k.gres["G"][lc]] if k.fused else []
        for hh, q in ((0, "sp"), (1, "act")):
            dst = KPE[hh * 64:(hh + 1) * 64, :].rearrange("p (r l t) -> p r l t", r=4, l=4)[:, :, lc, :]
            if k.fused:
                srcv = k.ext["G_c"][lc].rearrange("(r q) t -> q r t", q=768)[640:704]
            else:
                srcv = G[:, lc * CH:(lc + 1) * CH].rearrange("(r q) t -> q r t", q=768)[640:704]
            rk_ = Res()
            P.dma(q, dst, srcv, gr, [rk_])
            rKPE_all.append(rk_)

    QN = [A.alloc([2, CH], BF16) for _ in range(2)]
    QP = [A.alloc([CH], BF16) for _ in range(2)]
    rQ = [Res(), Res()]
    PT = [A.alloc([2 * CH], BF16) for _ in range(4)]
    rPT = [Res() for _ in range(4)]
    acc = [A.alloc([2 * CH], F32) for _ in range(2)]
    racc = [Res(), Res()]
    t1 = A.alloc([CH], F32)
    t2 = A.alloc([CH], F32)
    rt1, rt2 = Res(), Res()
    r1 = [A.alloc([CH], F32) for _ in range(2)]
    rr1 = [Res(), Res()]
    ob = [A.alloc([CH], BF16) for _ in range(2)]
    rob = [Res(), Res()]
    scale = 192 ** -0.5
    load_chunk(0, 0, True)
    cnt = 0
    piece_res = []
    wpre = WPrefetch(k, k.ext["wz_next"], k.ext["wo_next"]) if (k.fused and os.environ.get("KPRE", "1") == "1") else None
    for qb in range(16):
        i = qb % 2
        if qb + 1 < 16:
            load_chunk(qb + 1, 1 - i, True)
        if wpre is not None and qb >= 8:
            wpre.step()
        for h in range(2):
            for j in range(3):
                k.mm(k.bank(h), Wqn[:, j, h * 128:(h + 1) * 128], gb[i][:, j, :], j == 0, j == 2,
                     [rW, rgb[i]], [k.rb[h]], signal=(j == 2))
            k.copy("dve", QN[i][:, h, :], k.bank(h), [k.rb[h]], [rQ[i]])
        for (bk, W) in ((2, Wqp), (3, Wqpr)):
            for j in range(3):
                k.mm(k.bank(bk), W[:, j, :], gb[i][:, j, :], j == 0, j == 2, [rW, rgb[i]], [k.rb[bk]], signal=(j == 2))
        k.tt("dve", t1, k.bank(2), cb[i], ALU.mult, [k.rb[2], rtb[i]], [rt1])
        k.tt("dve", t2, k.bank(3), sb_[i], ALU.mult, [k.rb[3], rtb2[i]], [rt2])
        k.tt("dve", QP[i], t1, t2, ALU.add, [rt1, rt2], [rQ[i]])
        qn, qp, rq = QN[i], QP[i], rQ[i]

        def qk_fn(u, b0, qn=qn, qp=qp, rq=rq):
            ksl = slice(u * 128, (u + 1) * 128)
            k.mm(k.bank(b0), KPE[0:64, ksl], qp[0:64, :], True, False, rKPE_all + [rq], [k.rb[b0]], signal=False)
            k.mm(k.bank(b0 + 1), KPE[64:128, ksl], qp[64:128, :], True, False, rKPE_all + [rq], [k.rb[b0 + 1]], signal=False)
            k.mm(k.bank(b0), KT[:, 0, ksl], qn[:, 0, :], False, True, [rKT, rq], [k.rb[b0]], signal=False)
            k.mm(k.bank(b0 + 1), KT[:, 1, ksl], qn[:, 1, :], False, True, [rKT, rq], [k.rb[b0 + 1]], signal=True)

        aset = qb % 2

        def pv_fn(u, pt, rpt, aset=aset):
            st, sp = (u == 0), (u == 63)
            k.mm(k.bank(4), V[:, u, 0:128], pt[:, 0:CH], st, sp, [rV, rpt], [k.rb[4]], signal=False)
            k.mm(k.bank(5), V[:, u, 128:256], pt[:, CH:2 * CH], st, sp, [rV, rpt], [k.rb[5]], signal=True)
            if st:
                k.copy("dve", acc[aset], pt, [rpt], [racc[aset]])
            else:
                k.tt("dve", acc[aset], acc[aset], pt, ALU.add, [rpt, racc[aset]], [racc[aset]])

        attention_block(k, 64, qk_fn, pv_fn, scale, PT, rPT, warm0=int(os.environ.get("KWARM", "20")), warm_bank=0)
        for h in range(2):
            k.mm(k.bank(6 + h), k.ones_f, acc[aset][:, h * CH:(h + 1) * CH], True, True, [k.cres, racc[aset]], [k.rb[6 + h]])
        for h in range(2):
            k.recip(r1[h], k.bank(6 + h), [k.rb[6 + h]], [rr1[h]])
            j = cnt % 2
            cnt += 1
            k.tt("dve", ob[j], k.bank(4 + h), r1[h], ALU.mult, [k.rb[4 + h], rr1[h]], [rob[j]])
            rpc = Res()
            P.dma("sp", oT_pc(h, qb), ob[j], [rob[j]], [rpc])
            piece_res.append(rpc)
        if k.fused and qb % 4 == 3:
            k.gather_piece("OT", qb // 4, k.ext["oT_c"][qb // 4], k.ext["OT_dst"][qb // 4], piece_res)
            piece_res = []
    return k.done()


def build_L5(k=None):
    k = k or K("L5")
    x = k.din("x", [TO, D])
    oT = k.din("oT", [D, TO], BF16)
    hT1 = k.din("hT1", [D, TO], BF16)
    wz = k.din("wz", [D, D])
    wo = k.din("wo", [D, D])
    modrow1 = k.din("modrow1", [1, 3 * D])
    fin_g = k.din("fin_g", [1, D])
    out = k.dout("out", [TO, D])
    A, P = k.A, k.P
    k.load_consts(None, need_ident=False)
    gf = A.alloc([D], F32)
    rgf = Res()
    P.dma("sp", gf, fin_g[0:1, :].partition_broadcast(128), [], [rgf])
    junk = A.alloc([D], F32)
    rjunk = Res()
    ss = A.alloc([16 * 16], F32)
    rss = [Res() for _ in range(16)]
    P.op("pool", lambda e: e.memset(ss, 0.0), [], rss)
    yo = [A.alloc([D], F32) for _ in range(2)]
    ryo = [Res(), Res()]

    def cb0(tt, xn, rxn):
        k.act(junk, xn, AF.Square, [rxn, rjunk], [rss[tt], rjunk], accum_out=ss[:, tt * 16:tt * 16 + 1])

    def cb1(tt):
        k.rsqrt_inplace(ss[:, tt * 16:tt * 16 + 1], rss[tt], 1.0 / D, EPS)

    def cb2(tt, xn, rxn):
        j = tt % 2
        k.stt("dve", yo[j], xn, ss[:, tt * 16:tt * 16 + 1], gf, ALU.mult, ALU.mult, [rxn, rss[tt], rgf], [ryo[j]])
        P.dma("act", out[tt * 128:(tt + 1) * 128, :], yo[j], [ryo[j]], [])

    tile_cb = (cb0, cb1, cb2)

    phase_C(k, 1, x, oT, hT1, wz, wo, modrow1, Res(), None, tile_cb)
    return k.done()


def all_gather(k, src, dst):
    P = k.P
    P.barrier()
    P.op("pool", lambda e: e.collective_compute("AllGather", ALU.bypass, replica_groups=GROUPS, ins=[src.opt()], outs=[dst.opt()]), [], [])
    P.barrier()


def build_fused():
    k = K("F")
    k.fused = True
    nc = k.nc
    I = lambda n, s, dt=F32: nc.dram_tensor(n, list(s), dt, kind="ExternalInput").ap()
    T = lambda n, s, dt=F32: nc.dram_tensor(n, list(s), dt).ap()
    x = I("x", [TO, D]); cvec = I("cvec", [128, KC]); pos = I("pos", [1, S], I32); pos_own = I("pos_own", [1, TO], I32)
    cst = I("cst", [128, 8]); ident = I("ident", [128, 128]); sel = I("sel", [128, 4])
    ada_w0 = I("ada_w0", [D, 3 * D]); ada_b0 = I("ada_b0", [1, 3 * D]); norm_g0 = I("norm_g0", [1, D])
    wq0 = I("wq0", [D, 256]); wk0 = I("wk0", [D, 256]); wv0 = I("wv0", [D, 256]); lamv = I("lamv", [1, 256])
    wz0 = I("wz0", [D, D]); wo0 = I("wo0", [D, D]); subln = I("subln", [128, 1])
    ada_w1 = I("ada_w1", [D, 3 * D]); ada_b1 = I("ada_b1", [1, 3 * D]); norm_g1 = I("norm_g1", [1, D])
    w1a = I("w1a", [D, 704]); gqkv = I("gqkv", [128, 5])
    wqn = I("wqn", [384, 256]); wqp = I("wqp", [384, 128]); wkk = I("wkk", [256, 256]); wkv = I("wkv", [256, 256])
    wz1 = I("wz1", [D, D]); wo1 = I("wo1", [D, D]); fin_g = I("fin_g", [1, D])
    out = nc.dram_tensor("out", [TO, D], F32, kind="ExternalOutput").ap()
    hT0_c = [T("hT0_c%d" % i, [D, CH], BF16) for i in range(4)]
    HT0_c = [T("HT0_c%d" % i, [4 * D, CH], BF16) for i in range(4)]
    oT_c = [T("oT_c%d" % i, [256, TO], BF16) for i in range(4)]
    OT_c = [T("OT_c%d" % i, [D, TO], BF16) for i in range(4)]
    hT1_c = [T("hT1_c%d" % i, [D, CH], BF16) for i in range(4)]
    GT_c = [T("GT_c%d" % i, [768, CH], BF16) for i in range(4)]
    G_c = [T("G_c%d" % i, [4 * 768, CH], BF16) for i in range(4)]
    modrow0 = T("modrow0", [1, 3 * D]); modrow1 = T("modrow1", [1, 3 * D]); x1 = T("x1", [TO, D])
    cosT = T("cosT_all", [128, S]); sinT = T("sinT_all", [128, S]); cosO = T("cosT_own", [128, TO]); sinO = T("sinT_own", [128, TO])
    dummy = T("dummy_x", [1, 64])
    k.load_consts(ident)
    k.Wz_pre = k.A.alloc([KC, D], BF16)
    k.Wo_pre = k.A.alloc([KC, D], BF16)
    k.rWpre = Res()
    k.base_mark = k.A.mark()

    def setE(**kw):
        k.ext.clear()
        k.ext["sel"] = sel
        k.ext.update(kw)

    for nm in ("HT", "OT", "G"):
        k.gres[nm] = [Res() for _ in range(4)]

    stop = int(os.environ.get("KSTOP", "99"))
    setE(x=x, cvec=cvec, ada_w=ada_w0, ada_b=ada_b0, norm_g=norm_g0, ident=ident, hT=dummy, hT_c=hT0_c, HT_dst=HT0_c, modrow=modrow0,
         early=dict(pos=pos, pos_own=pos_own, cst=cst, cosT=cosT, sinT=sinT, cosO=cosO, sinO=sinO,
                    ada_w1=ada_w1, ada_b1=ada_b1, norm_g1=norm_g1, modrow1=modrow1))
    build_L1(k)
    if stop == 0:
        return k.finish()
    if stop == 1:
        return k.finish()
    setE(HT=dummy, HT_c=HT0_c, pos=pos, cst=cst, wq=wq0, wk=wk0, wv=wv0, lamv=lamv, ident=ident, oT=dummy, oT_c=oT_c, OT_dst=OT_c, cosT=cosT, sinT=sinT, wz_next=wz0, wo_next=wo0,
         early=dict(pos_own=pos_own, cosO=cosO, sinO=sinO, cvec=cvec, ada_w1=ada_w1, ada_b1=ada_b1, norm_g1=norm_g1, modrow1=modrow1))
    build_L2(k)
    if stop == 2:
        return k.finish()
    if stop == 3:
        return k.finish()
    setE(x=x, oT=OT_c, hT0=hT0_c, wz=wz0, wo=wo0, modrow0=modrow0, subln=subln, cvec=cvec, ada_w=ada_w1, ada_b=ada_b1,
         norm_g=norm_g1, w1a=w1a, gqkv=gqkv, pos_own=pos_own, cst=cst, ident=ident, x1=x1, hT1=dummy, hT1_c=hT1_c, GT=dummy, GT_c=GT_c, G_dst=G_c,
         modrow1=modrow1, cosT=cosO, sinT=sinO)
    build_L3(k)
    if stop == 4:
        return k.finish()
    setE(G=dummy, G_c=G_c, pos=pos, cst=cst, wqn=wqn, wqp=wqp, wkk=wkk, wkv=wkv, oT=dummy, oT_c=oT_c, OT_dst=OT_c, cosT=cosT, sinT=sinT, wz_next=wz1, wo_next=wo1)
    build_L4(k)
    setE(x=x1, oT=OT_c, hT1=hT1_c, wz=wz1, wo=wo1, modrow1=modrow1, fin_g=fin_g, out=out)
    build_L5(k)
    return k.finish()


_NC = {}


def _get(name, fn):
    if name not in _NC:
        _NC[name] = fn()
    return _NC[name]


def _run(nc, in_maps):
    res = run_bass_kernel_spmd(nc, in_maps, core_ids=list(range(8)))
    return res.results


def kernel(x, c, positions,
           ada_w0, ada_b0, norm_g0, w_in0, lam_q1, lam_k1, lam_q2, lam_k2, subln_g, w_out0,
           ada_w1, ada_b1, norm_g1, w_in1, q_a_norm_g, w_q_b, kv_a_norm_g, w_kv_b, w_out1,
           final_norm_g):
    f = lambda a: np.ascontiguousarray(np.asarray(a), dtype=np.float32)
    x = f(x); c = f(c)
    positions = np.ascontiguousarray(np.asarray(positions), dtype=np.int32)
    ada_w0, ada_b0, norm_g0, w_in0, w_out0 = map(f, (ada_w0, ada_b0, norm_g0, w_in0, w_out0))
    ada_w1, ada_b1, norm_g1, w_in1, w_out1 = map(f, (ada_w1, ada_b1, norm_g1, w_in1, w_out1))
    w_q_b, w_kv_b = f(w_q_b), f(w_kv_b)
    ident = np.eye(128, dtype=np.float32)
    cst = np.zeros((128, 8), np.float32)
    cst[:, 0] = (10000.0 ** (-(np.arange(128) % 32).astype(np.float32) / 32.0)).astype(np.float32)
    lamv = np.concatenate([f(lam_q1), f(lam_k1), f(lam_q2), f(lam_k2)])[None, :]
    gqkv = np.ascontiguousarray(np.concatenate([f(q_a_norm_g).reshape(3, 128).T, f(kv_a_norm_g).reshape(2, 128).T], 1))
    wqb = w_q_b.reshape(384, 8, 192)
    wkvb = w_kv_b.reshape(256, 8, 256)
    ca = np.ascontiguousarray
    shared = dict(cst=cst, ident=ident, ada_w0=ada_w0, ada_b0=ada_b0[None, :], norm_g0=norm_g0[None, :], lamv=lamv,
                  wz0=ca(w_in0[:, 3072:4096]), wo0=w_out0, subln=f(subln_g)[:, None],
                  ada_w1=ada_w1, ada_b1=ada_b1[None, :], norm_g1=norm_g1[None, :], w1a=ca(w_in1[:, 0:704]), gqkv=gqkv,
                  wz1=ca(w_in1[:, 704:1728]), wo1=w_out1, fin_g=f(final_norm_g)[None, :])
    in_maps = []
    for cc in range(8):
        b, j = cc // 4, cc % 4
        sl = slice(j * 256, (j + 1) * 256)
        hs = slice(2 * j, 2 * j + 2)
        selv = np.zeros((128, 4), np.float32)
        selv[:, j] = 1.0
        m = dict(shared)
        m.update(x=ca(x[b, j * TO:(j + 1) * TO]), cvec=ca(c[b].reshape(KC, 128).T), pos=ca(positions[b][None, :]),
                 pos_own=ca(positions[b][None, j * TO:(j + 1) * TO]), sel=selv,
                 wq0=ca(w_in0[:, 0:1024][:, sl]), wk0=ca(w_in0[:, 1024:2048][:, sl]), wv0=ca(w_in0[:, 2048:3072][:, sl]),
                 wqn=ca(wqb[:, hs, 0:128].reshape(384, 256)), wqp=ca(wqb[:, hs, 128:192].reshape(384, 128)),
                 wkk=ca(wkvb[:, hs, 0:128].reshape(256, 256)), wkv=ca(wkvb[:, hs, 128:256].reshape(256, 256)))
        in_maps.append(m)
    res = _run(_get("F", build_fused), in_maps)
    out = np.empty((2, S, D), np.float32)
    for cc in range(8):
        b, j = cc // 4, cc % 4
        out[b, j * TO:(j + 1) * TO] = res[cc]["out"]
    return out
```
